# Optimizing a Trainium2 kernel written in Bass

```python
import jax, jax.numpy as jnp
from jax import lax
import numpy as np

D_MODEL = 1024
BATCH = 8
SEQ = 8192
DEPTH = 1
DEC_BATCH = 16
DEC_SEQ = 4096
PAST_LEN = 128

GRID_W = 64
N_MEM = 256
F_GROUPS = 4
F_GROUP_DIM = 96
F_WIDTH = F_GROUPS * F_GROUP_DIM
NA_HEADS = 6
NA_HEAD_DIM = 64
NA_WIDTH = NA_HEADS * NA_HEAD_DIM
CA_HEADS = 4
CA_HEAD_DIM = 64
CA_WIDTH = CA_HEADS * CA_HEAD_DIM
NA_KH_MAX = 8
NA_KW = 16
NA_QBLK = 16
NA_KBLK = 32
N_BRANCH = 3
IN_SPLITS = (F_WIDTH, F_WIDTH, 3 * NA_WIDTH, NA_WIDTH, CA_WIDTH, CA_WIDTH, N_BRANCH * D_MODEL)
IN_WIDTH = F_WIDTH * 2 + NA_WIDTH * 4 + CA_WIDTH * 2 + N_BRANCH * D_MODEL
EPS = 1e-6
NEG = -1e30

kernel_name = "hybrid_fnet_natten_memory_encoder"


def rmsnorm(x, g):
    xf = x.astype(jnp.float32)
    y = xf * lax.rsqrt(jnp.mean(xf * xf, axis=-1, keepdims=True) + EPS)
    return (y * g.astype(jnp.float32)).astype(x.dtype)


def fourier_mix(u):
    B, S, _ = u.shape
    ug = u.reshape(B, S, F_GROUPS, F_GROUP_DIM).astype(jnp.float32)
    yr = jnp.fft.fft2(ug, axes=(1, 3), norm="ortho").real
    return yr.reshape(B, S, F_WIDTH).astype(u.dtype)


def na_indices(rows):
    kh = min(NA_KH_MAX, rows)
    r = np.arange(rows)
    rs = np.clip(r - kh // 2, 0, rows - kh)
    row_idx = rs[:, None] + np.arange(kh)[None, :]
    dr = row_idx - r[:, None] + (NA_KH_MAX - 1)
    nj = GRID_W // NA_QBLK
    j = np.arange(nj)
    c0 = np.clip(j * NA_QBLK - NA_KW // 2, 0, GRID_W - NA_KBLK)
    col_idx = c0[:, None] + np.arange(NA_KBLK)[None, :]
    c = j[:, None] * NA_QBLK + np.arange(NA_QBLK)[None, :]
    cs = np.clip(c - NA_KW // 2, 0, GRID_W - NA_KW)
    rel = col_idx[:, None, :] - cs[:, :, None]
    valid = (rel >= 0) & (rel < NA_KW)
    dc = np.clip(col_idx[:, None, :] - c[:, :, None] + (NA_KW - 1), 0, 2 * NA_KW - 2)
    return row_idx, dr, col_idx, valid, dc


def neighborhood_attention(q, k, v, rpb):
    B, S, H, Dh = q.shape
    rows = S // GRID_W
    row_idx, dr, col_idx, valid, dc = na_indices(rows)
    nj = GRID_W // NA_QBLK
    qb = q.reshape(B, rows, nj, NA_QBLK, H, Dh)
    ridx = row_idx[:, None, :, None]
    cidx = col_idx[None, :, None, :]
    kg = k.reshape(B, rows, GRID_W, H, Dh)[:, ridx, cidx]
    vg = v.reshape(B, rows, GRID_W, H, Dh)[:, ridx, cidx]
    s = jnp.einsum('brjqhd,brjkwhd->bhrjqkw', qb, kg).astype(jnp.float32) * (Dh ** -0.5)
    bias = rpb.astype(jnp.float32)[:, dr[:, None, None, :, None], dc[None, :, :, None, :]]
    mask = jnp.asarray(valid)[None, :, :, None, :]
    s = jnp.where(mask, s + bias[None], NEG)
    sh = s.shape
    p = jax.nn.softmax(s.reshape(sh[:5] + (sh[5] * sh[6],)), axis=-1).reshape(sh)
    o = jnp.einsum('bhrjqkw,brjkwhd->brjqhd', p.astype(v.dtype), vg)
    return o.reshape(B, S, H * Dh)


def memory_attention(q, k, v):
    B, S, H, Dh = q.shape
    s = jnp.einsum('bshd,bmhd->bhsm', q, k).astype(jnp.float32) * (Dh ** -0.5)
    p = jax.nn.softmax(s, axis=-1).astype(v.dtype)
    return jnp.einsum('bhsm,bmhd->bshd', p, v).reshape(B, S, H * Dh)


def hybrid_layer(x, mem, g_norm, w_in, rpb, g_mem, w_mem_kv, w_f_out, w_na_out, w_ca_out, w_out):
    B, S, D = x.shape
    h = rmsnorm(x, g_norm)
    z = h @ w_in
    cuts = [int(c) for c in np.cumsum(IN_SPLITS)[:-1]]
    u_f, gate_f, qkv_na, gate_na, q_ca, gate_ca, g_merge = jnp.split(z, cuts, axis=-1)
    y_f = (fourier_mix(u_f) * jax.nn.silu(gate_f)) @ w_f_out
    qkv = qkv_na.reshape(B, S, 3, NA_HEADS, NA_HEAD_DIM)
    o_na = neighborhood_attention(qkv[:, :, 0], qkv[:, :, 1], qkv[:, :, 2], rpb)
    y_na = (o_na * jax.nn.silu(gate_na)) @ w_na_out
    M = mem.shape[1]
    kv = (rmsnorm(mem, g_mem) @ w_mem_kv).reshape(B, M, 2, CA_HEADS, CA_HEAD_DIM)
    o_ca = memory_attention(q_ca.reshape(B, S, CA_HEADS, CA_HEAD_DIM), kv[:, :, 0], kv[:, :, 1])
    y_ca = (o_ca * jax.nn.silu(gate_ca)) @ w_ca_out
    gm = jax.nn.sigmoid(g_merge).reshape(B, S, N_BRANCH, D)
    merged = gm[:, :, 0] * y_f + gm[:, :, 1] * y_na + gm[:, :, 2] * y_ca
    return x + merged @ w_out


def trunk(x, mem, g_norm, w_in, na_rpb, g_mem, w_mem_kv, w_f_out, w_na_out, w_ca_out, w_out, g_final):
    for l in range(DEPTH):
        x = hybrid_layer(x, mem, g_norm[l], w_in[l], na_rpb[l], g_mem[l], w_mem_kv[l],
                         w_f_out[l], w_na_out[l], w_ca_out[l], w_out[l])
    return rmsnorm(x, g_final)


def setup_inputs(seed: int = 0) -> dict:
    key = jax.random.key(seed)
    ks = jax.random.split(key, 16)
    f32 = jnp.float32
    nrm = lambda k, shape, s: jax.random.normal(k, shape, f32) * s
    return {
        "x_prompt": nrm(ks[0], (BATCH, SEQ, D_MODEL), 1.0),
        "x_sample": nrm(ks[1], (DEC_BATCH, DEC_SEQ, D_MODEL), 1.0),
        "mem_prompt": nrm(ks[2], (BATCH, N_MEM, D_MODEL), 1.0),
        "mem_sample": nrm(ks[3], (DEC_BATCH, N_MEM, D_MODEL), 1.0),
        "g_norm": 1.0 + nrm(ks[4], (DEPTH, D_MODEL), 0.02),
        "w_in": nrm(ks[5], (DEPTH, D_MODEL, IN_WIDTH), D_MODEL ** -0.5),
        "na_rpb": nrm(ks[6], (DEPTH, NA_HEADS, 2 * NA_KH_MAX - 1, 2 * NA_KW - 1), 0.1),
        "g_mem": 1.0 + nrm(ks[7], (DEPTH, D_MODEL), 0.02),
        "w_mem_kv": nrm(ks[8], (DEPTH, D_MODEL, 2 * CA_WIDTH), D_MODEL ** -0.5),
        "w_f_out": nrm(ks[9], (DEPTH, F_WIDTH, D_MODEL), F_WIDTH ** -0.5),
        "w_na_out": nrm(ks[10], (DEPTH, NA_WIDTH, D_MODEL), NA_WIDTH ** -0.5),
        "w_ca_out": nrm(ks[11], (DEPTH, CA_WIDTH, D_MODEL), CA_WIDTH ** -0.5),
        "w_out": nrm(ks[12], (DEPTH, D_MODEL, D_MODEL), D_MODEL ** -0.5),
        "g_final": 1.0 + nrm(ks[13], (D_MODEL,), 0.02),
    }


def reference(x_prompt, x_sample, mem_prompt, mem_sample, g_norm, w_in, na_rpb, g_mem, w_mem_kv,
              w_f_out, w_na_out, w_ca_out, w_out, g_final):
    y_prompt = trunk(x_prompt, mem_prompt, g_norm, w_in, na_rpb, g_mem, w_mem_kv,
                     w_f_out, w_na_out, w_ca_out, w_out, g_final)
    y_sample = trunk(x_sample, mem_sample, g_norm, w_in, na_rpb, g_mem, w_mem_kv,
                     w_f_out, w_na_out, w_ca_out, w_out, g_final)
    return (y_prompt, y_sample)
```

```python
import numpy as np
from contextlib import ExitStack
import concourse.bass as bass
import concourse.mybir as mybir
from concourse.bass_utils import run_bass_kernel_spmd

F32 = mybir.dt.float32
BF16 = mybir.dt.bfloat16
AF = mybir.ActivationFunctionType
ALU = mybir.AluOpType

D = 1024
EPS = 1e-6
NCORES = 8
GM_OFF = 2432
OFF = dict(gf=0, q=384, k=768, v=1152, gna=1536, qca=1920, gca=2176)


USE_RANK = True
PSUM_SLOTS = {'pT', 'pA', 'pB', 'pS0', 'pS1', 'pS2', 'pS3', 'pO'}


class Slot:
    def __init__(self, name):
        self.name = name
        self.w = []
        self.r = []


class _Probe:
    def __init__(self):
        self.rec = None

    def __getattr__(self, name):
        def f(*a, **k):
            self.rec = (name, a, k)
            return self
        return f


def _free_size(ap):
    n = 1
    for d in ap.shape[1:]:
        n *= int(d)
    return n


def _auto_cost(eng, fn):
    p = _Probe()
    try:
        fn(p)
        name, a, k = p.rec
        out = k.get('out', a[0] if a else None)
        n = _free_size(out)
        if eng == 'pe':
            if name == 'transpose':
                return 0.085
            return max(0.057, n / 2400.0 + 0.012)
        if eng == 'act':
            return 0.17 + n / 1200.0 + (0.1 if 'accum_out' in k else 0.0)
        if eng == 'dve':
            return 0.1 + n / 1400.0
        if eng == 'pool':
            return 0.25 + n / 450.0
    except Exception:
        pass
    return None


class Trk:
    ENG = ['pe', 'act', 'dve', 'pool', 'sp']
    COST = {'pe': 0.12, 'act': 0.6, 'dve': 0.5, 'pool': 0.9, 'sp': 2.5}
    HOP = 0.2
    HOP_SAME = 0.08
    WINDOW = 120

    def __init__(self, nc, es):
        self.nc = nc
        self.es = es
        self.psem = {e: es.enter_context(nc.semaphore('prog_' + e)) for e in ['pe', 'act', 'dve', 'pool']}
        self.cnt = {e: 0 for e in self.psem}
        self.dsem = {}
        self.dcnt = {}
        self.slots = {}
        self.units = []
        self.pend = {}
        self.waited = {e: {} for e in self.ENG}
        self.reorder = True
        self.use_rank = USE_RANK
        self.diag = None

    def S(self, name):
        if name not in self.slots:
            self.slots[name] = Slot(name)
        return self.slots[name]

    def _finish_unit(self, eng, fns, reads, writes, cost, semname=None):
        uid = len(self.units)
        deps = set()
        for s in reads:
            deps.update(s.w)
            if s.name in PSUM_SLOTS:
                deps.update(s.r)
        for s in writes:
            deps.update(s.w)
            deps.update(s.r)
        deps.discard(uid)
        self.units.append(dict(eng=eng, fns=fns, deps=deps, cost=cost, sem=semname, rn=[x.name for x in reads], wn=[x.name for x in writes]))
        for s in reads:
            s.r.append(uid)
        for s in writes:
            s.w = [uid]
            s.r = []
        return uid

    def op(self, eng, fn, reads=(), writes=(), signal=True, cost=None):
        reads = [self.S(n) for n in reads]
        writes = [self.S(n) for n in writes]
        p = self.pend.setdefault(eng, dict(fns=[], reads=[], writes=[], cost=0.0))
        p['fns'].append(fn)
        p['reads'] += [x for x in reads if x not in p['reads']]
        p['writes'] += [x for x in writes if x not in p['writes']]
        if cost is None:
            cost = _auto_cost(eng, fn)
        p['cost'] += (self.COST[eng] if cost is None else cost)
        if signal:
            del self.pend[eng]
            self._finish_unit(eng, p['fns'], p['reads'], p['writes'], p['cost'])

    def dma(self, eng, out, in_, semname, reads=(), writes=(), cost=None):
        reads = [self.S(n) for n in reads]
        writes = [self.S(n) for n in writes]
        if semname not in self.dsem:
            self.dsem[semname] = self.es.enter_context(self.nc.semaphore('d_' + semname))
            self.dcnt[semname] = 0
        fn = lambda e, out=out, in_=in_: e.dma_start(out=out, in_=in_)
        if cost is None:
            try:
                cost = 2.0 + out.nbytes() / 250e3
            except Exception:
                cost = self.COST['sp']
        self._finish_unit(eng, [fn], reads, writes, cost, semname=semname)

    def barrier(self):
        pass

    def _schedule(self):
        U = self.units
        n = len(U)
        order = {e: [] for e in self.ENG}
        if not self.reorder:
            for i, u in enumerate(U):
                order[u['eng']].append(i)
            return order
        ndep = [len(u['deps']) for u in U]
        users = [[] for _ in range(n)]
        for i, u in enumerate(U):
            for d in u['deps']:
                users[d].append(i)
        ready = [0.0 if ndep[i] == 0 else None for i in range(n)]
        fin = [None] * n
        queue = {e: [] for e in self.ENG}
        for i, u in enumerate(U):
            queue[u['eng']].append(i)
        head = {e: 0 for e in self.ENG}
        done = [False] * n
        free_at = {e: 0.0 for e in self.ENG}
        left = n
        W = self.WINDOW
        rank = [0.0] * n
        if self.use_rank:
            for i in range(n - 1, -1, -1):
                r = 0.0
                for k in users[i]:
                    v = rank[k] + (self.HOP_SAME if U[k]['eng'] == U[i]['eng'] else self.HOP)
                    if v > r:
                        r = v
                rank[i] = r + U[i]['cost']
        while left:
            best = None
            for e in self.ENG:
                q = queue[e]
                h = head[e]
                while h < len(q) and done[q[h]]:
                    h += 1
                head[e] = h
                fa = free_at[e]
                cnt = 0
                j = h
                cand = None
                while j < len(q) and cnt < W:
                    i = q[j]
                    j += 1
                    if done[i]:
                        continue
                    cnt += 1
                    r = ready[i]
                    if r is None:
                        continue
                    st = r if r > fa else fa
                    key = (st, -rank[i], i)
                    if cand is None or key < cand:
                        cand = key
                    if not self.use_rank and r <= fa:
                        break
                if cand is not None:
                    key = (cand[0], cand[1], cand[2], e)
                    if best is None or key[:3] < best[:3]:
                        best = key
            best = (best[0], best[2], best[3])
            st, i, e = best
            if self.diag is not None and st > free_at[e] + 0.2 and U[i]['deps']:
                d = max(U[i]['deps'], key=lambda d: fin[d])
                key = (e, U[d]['eng'], tuple(U[d]['wn'][:2]), tuple(U[i]['wn'][:1]))
                self.diag[key] = self.diag.get(key, 0.0) + (st - free_at[e])
            f = st + U[i]['cost']
            fin[i] = f
            free_at[e] = (st + 0.15) if U[i]['sem'] is not None else f
            done[i] = True
            order[e].append(i)
            left -= 1
            for k in users[i]:
                ndep[k] -= 1
                if ndep[k] == 0:
                    ek = U[k]['eng']
                    ready[k] = max(fin[d] + (self.HOP_SAME if U[d]['eng'] == ek else self.HOP) for d in U[k]['deps'])
        self.sim_time = max(free_at.values())
        return order

    def emit(self):
        nc = self.nc
        assert not self.pend, self.pend.keys()
        U = self.units
        order = self._schedule()
        self.stats = getattr(self, 'stats', []) + [(len(U), {e: len(order[e]) for e in self.ENG}, getattr(self, 'sim_time', None))]
        tok = [None] * len(U)
        for e in self.ENG:
            for i in order[e]:
                u = U[i]
                if u['sem'] is not None:
                    self.dcnt[u['sem']] += 16
                    tok[i] = (u['sem'], self.dcnt[u['sem']])
                else:
                    self.cnt[e] += 1
                    tok[i] = (e, self.cnt[e])
        lists = {e: [] for e in self.ENG}
        for e in self.ENG:
            L = lists[e]
            wd = self.waited[e]
            for i in order[e]:
                u = U[i]
                need = {}
                for d in u['deps']:
                    k, v = tok[d]
                    if v > need.get(k, 0):
                        need[k] = v
                for k, v in need.items():
                    if wd.get(k, 0) >= v:
                        continue
                    wd[k] = v
                    sem = self.psem[k] if k in self.psem else self.dsem[k]
                    L.append(lambda en, sem=sem, v=v: en.wait_ge(sem, v))
                fns = u['fns']
                for fn in fns[:-1]:
                    L.append(fn)
                if u['sem'] is not None:
                    sem = self.dsem[u['sem']]
                    L.append(lambda en, fn=fns[-1], sem=sem: fn(en).then_inc(sem, 16))
                else:
                    sem = self.psem[e]
                    L.append(lambda en, fn=fns[-1], sem=sem: fn(en).then_inc(sem, 1))
            for k, v in list(self.cnt.items()) + list(self.dcnt.items()):
                if v > 0 and wd.get(k, 0) < v:
                    wd[k] = v
                    sem = self.psem[k] if k in self.psem else self.dsem[k]
                    L.append(lambda en, sem=sem, v=v: en.wait_ge(sem, v))
        with nc.Block() as block:
            @block.tensor
            def _(en):
                for f in lists['pe']:
                    f(en)

            @block.scalar
            def _(en):
                for f in lists['act']:
                    f(en)

            @block.vector
            def _(en):
                for f in lists['dve']:
                    f(en)

            @block.gpsimd
            def _(en):
                for f in lists['pool']:
                    f(en)

            @block.sync
            def _(en):
                for f in lists['sp']:
                    f(en)
        self.units = []
        for s in self.slots.values():
            s.w = []
            s.r = []


def _consts():
    c = {}
    a = np.arange(96)
    ang = 2 * np.pi * np.outer(a, a) / 96
    c['cs96'] = np.concatenate([np.cos(ang), -np.sin(ang)], 1).astype(np.float32)
    r = np.arange(128)
    ang = 2 * np.pi * np.outer(r, r) / 128
    sc = 1.0 / np.sqrt(96.0 * 8192.0)
    C, S_ = np.cos(ang) * sc, np.sin(ang) * sc
    c['f1a_p'] = np.concatenate([C, -S_], 1).astype(np.float32)
    c['f1b_p'] = np.concatenate([S_, C], 1).astype(np.float32)
    r64 = np.arange(64)
    ang = 2 * np.pi * np.outer(r64, r64) / 64
    sc = 1.0 / np.sqrt(96.0 * 4096.0)
    C64, S64 = np.cos(ang) * sc, np.sin(ang) * sc
    Cb = np.zeros((128, 128)); Sb = np.zeros((128, 128))
    for b in range(2):
        Cb[64 * b:64 * b + 64, 64 * b:64 * b + 64] = C64
        Sb[64 * b:64 * b + 64, 64 * b:64 * b + 64] = S64
    c['f1a_s'] = np.concatenate([Cb, -Sb], 1).astype(np.float32)
    c['f1b_s'] = np.concatenate([Sb, Cb], 1).astype(np.float32)
    m = np.arange(128)
    cc = m // 2
    th_p = 2 * np.pi * np.outer(cc, np.arange(128)) / 8192.0
    th_s = 2 * np.pi * np.outer(cc, np.arange(128) % 64) / 4096.0
    for nm, th in (('p', th_p), ('s', th_s)):
        tc = np.cos(th); ts = np.sin(th)
        tcc = np.stack([np.stack([tc, tc], 1)] * 2, 1)
        tsp = np.stack([np.stack([ts, -ts], 1)] * 2, 1)
        c['tcc_' + nm] = tcc.astype(np.float32)
        c['tsp_' + nm] = tsp.astype(np.float32)
    ang = 2 * np.pi * np.outer(r64, r64) / 64
    C2 = np.zeros((64, 2, 64, 2)); S2 = np.zeros((64, 2, 64, 2))
    for l in range(2):
        C2[:, l, :, l] = np.cos(ang)
        S2[:, l, :, l] = np.sin(ang)
    c['c2'] = C2.reshape(128, 128).astype(np.float32)
    c['s2'] = S2.reshape(128, 128).astype(np.float32)
    c['ident'] = np.eye(128, dtype=np.float32)
    return c


def _etab_index():
    kc = np.arange(64)[:, None]
    qc = np.arange(64)[None, :]
    cs = np.clip(qc - 8, 0, 48)
    colvalid = ((kc - cs) >= 0) & ((kc - cs) < 16)
    dc = np.clip(kc - qc + 15, 0, 30)
    out = {}
    for nm, nj, c0, lo, hi in (('int', 10, 4, -4, 3), ('full', 14, 6, -7, 7)):
        dri = np.zeros((2, nj), np.int64)
        rv = np.zeros((2, nj), bool)
        for kh in range(2):
            for j in range(nj):
                dr = kh - j + c0
                rv[kh, j] = (lo <= dr <= hi)
                dri[kh, j] = np.clip(dr, -7, 7) + 7
        mask = (rv[:, None, :, None] & colvalid[None, :, None, :])
        out[nm] = (dri, mask.astype(np.float32), nj)
    return out, dc


def _build_etabs(rpb):
    idx, dc = _etab_index()
    res = {}
    for nm in ('int', 'full'):
        dri, mask, nj = idx[nm]
        g = rpb[:, dri[:, :, None, None], dc[None, None, :, :]]
        g = np.transpose(g, (1, 3, 0, 2, 4))
        g = g[:, :, [0, 2, 4, 1, 3, 5]]
        res['eb_' + nm] = np.ascontiguousarray(g.reshape(128, 6 * nj * 64)).astype(np.float32)
        m = np.broadcast_to(mask[:, :, None, :, :], (2, 64, 6, nj, 64))
        res['em_' + nm] = np.ascontiguousarray(m.reshape(128, 6 * nj * 64)).astype(np.float32)
    return res


def build_nc(stop=0, ntl=None, ncol=64, nj4=48, do_yd=True, kinds=('p', 's'), lvl=9, lvl2=9):
    nc = bass.Bass("TRN2", target_bir_lowering=False)
    dt_in = lambda name, shape: nc.dram_tensor(name, list(shape), F32, kind="ExternalInput").ap()
    xp = dt_in("xp", [8192, D])
    xs = dt_in("xs", [2, 4096, D])
    mem = dt_in("mem", [3, 256, D])
    gn = dt_in("gn", [128, 8])
    gmm = dt_in("gmm", [128, 8])
    gfin = dt_in("gfin", [128, D])
    w_in = dt_in("w_in", [D, 5888])
    w_kv = dt_in("w_kv", [D, 512])
    w_f = dt_in("w_f", [384, D])
    w_n = dt_in("w_n", [384, D])
    w_c = dt_in("w_c", [256, D])
    w_o = dt_in("w_o", [D, D])
    cst = {k: dt_in("c_" + k, v.shape) for k, v in _consts().items()}
    eb = {}
    for nm, nj in (('int', 10), ('full', 14)):
        eb['eb_' + nm] = dt_in("eb_" + nm, [128, 6 * nj * 64])
        eb['em_' + nm] = dt_in("em_" + nm, [128, 6 * nj * 64])
    yp = nc.dram_tensor("yp", [8192, D], F32, kind="ExternalOutput").ap()
    ys = nc.dram_tensor("ys", [2, 4096, D], F32, kind="ExternalOutput").ap()
    yd = nc.dram_tensor("yscr", [16384, 384], BF16).ap()

    with ExitStack() as es:
        T = Trk(nc, es)
        sb = lambda name, shape, dt: es.enter_context(nc.sbuf_tensor(name, shape, dt))
        ps = lambda name, shape, dt: es.enter_context(nc.psum_tensor(name, shape, dt))
        pTf = ps("pT", [128, 512], F32)
        pT = pTf[:].bitcast(BF16).rearrange("p (k t) -> p k t", k=8)
        pA = ps("pA", [128, 512], F32)
        pB = ps("pB", [128, 512], F32)
        pS = [ps("pS%d" % i, [128, 512], F32) for i in range(4)]
        pO = ps("pO", [128, 512], F32)
        ident = sb("ident", [128, 128], BF16)
        mh = sb("mh", [128, 1], F32)
        gn_t = sb("gn_t", [128, 8], F32)
        gmm_t = sb("gmm_t", [128, 8], F32)
        ss_l = [sb("ss%d" % i, [128, 1], F32) for i in range(2)]
        ssp_l = [sb("ssp%d" % i, [128, 1], F32) for i in range(2)]
        rstd_l = [sb("rstd%d" % i, [128, 1], F32) for i in range(2)]
        xn_l = [sb("xn0", [128, D], BF16), None]
        xn_names = ['xn0', 'xn1']
        xe = [sb("xe%d" % i, [128, D], F32) for i in range(2)]

        stg_h = [None, None, None, None]
        n_stg = [2]
        stg_w = [1536]

        def load_const(dst, src_ap, shape2, name):
            stg = stg_h[0]
            n = shape2
            o = 0
            while o < n:
                w = min(1536, n - o)
                T.dma('sp', stg[:, 0:w], src_ap[:, o:o + w], 'stg0', writes=['stg0'])
                T.op('dve', lambda e, o=o, w=w: e.tensor_copy(out=dst[:, o:o + w], in_=stg[:, 0:w]), reads=['stg0'], writes=[name])
                o += w

        T.op('dve', lambda e: e.memset(mh[:], -0.5), writes=['mh'])
        T.dma('sp', gn_t[:], gn[:, :], 'gn', writes=['gn'])
        T.dma('sp', gmm_t[:], gmm[:, :], 'gmm', writes=['gmm'])

        pT2 = pS[0][:].bitcast(BF16).rearrange("p (k t) -> p k t", k=8)

        def rms_to_hT(xt, xslot, hT_dst, hslot, par=0, ptv=None, ptn='pT', copy_eng='dve'):
            if ptv is None:
                ptv = pT
            xs_ = xslot if isinstance(xslot, list) else [xslot]
            xn, ss, ssp, rstd = xn_l[par], ss_l[par], ssp_l[par], rstd_l[par]
            xnn, ssn, sspn, rsn = xn_names[par], 'ss%d' % par, 'ssp%d' % par, 'rstd%d' % par
            T.op('act', lambda e: e.activation(out=xn[:], in_=xt[:], func=AF.Square, scale=1.0 / 32, accum_out=ss[:]), reads=xs_, writes=[xnn, ssn])
            T.op('dve', lambda e: e.tensor_scalar(out=ssp[:], in0=ss[:], scalar1=EPS, scalar2=None, op0=ALU.add), reads=[ssn], writes=[sspn])
            T.op('pool', lambda e: e.tensor_tensor(out=rstd[:], in0=ssp[:], in1=mh[:], op=ALU.pow), reads=[sspn, 'mh'], writes=[rsn])
            T.op('dve', lambda e: e.tensor_scalar(out=xn[:], in0=xt[:], scalar1=rstd[:, 0:1], scalar2=None, op0=ALU.mult), reads=xs_ + [rsn], writes=[xnn])
            for k in range(8):
                T.op('pe', lambda e, k=k: e.transpose(out=ptv[:, k, :], in_=xn[:, k * 128:(k + 1) * 128], identity=ident[:]), reads=[xnn, 'ident'], writes=[ptn], signal=(k == 7))
            if copy_eng == 'dve':
                T.op('dve', lambda e: e.tensor_copy(out=hT_dst, in_=ptv[:]), reads=[ptn], writes=[hslot])
            else:
                T.op('act', lambda e: e.activation(out=hT_dst, in_=ptv[:], func=AF.Copy), reads=[ptn], writes=[hslot])

        lw_ctr = [0]

        def load_weight(dst, src, nk, ncols, name, scale_t=None, scale_c=None, col0=0):
            cw = stg_w[0]
            for k in range(nk):
                o = 0
                while o < ncols:
                    w = min(cw, ncols - o)
                    bi = lw_ctr[0] % n_stg[0]
                    lw_ctr[0] += 1
                    stg = stg_h[bi]
                    sn = 'stg%d' % bi
                    T.dma('sp', stg[:, 0:w], src[k * 128:(k + 1) * 128, col0 + o:col0 + o + w], sn, writes=[sn])
                    po = 0
                    pi = 0
                    while po < w:
                        pw = min(1408, w - po)
                        eng = 'act' if (pi + bi) % 2 == 0 else 'dve'
                        pi += 1
                        oo = o + po
                        if eng == 'act':
                            if scale_t is not None:
                                T.op('act', lambda e, k=k, oo=oo, po=po, pw=pw, stg=stg: e.activation(out=dst[:, k, oo:oo + pw], in_=stg[:, po:po + pw], func=AF.Copy, scale=scale_t[:, k:k + 1]), reads=[sn, 'gn', 'gmm'], writes=[name + '_a'])
                            else:
                                sc = 1.0 if scale_c is None else float(scale_c)
                                T.op('act', lambda e, k=k, oo=oo, po=po, pw=pw, stg=stg, sc=sc: e.activation(out=dst[:, k, oo:oo + pw], in_=stg[:, po:po + pw], func=AF.Copy, scale=sc), reads=[sn], writes=[name + '_a'])
                        else:
                            if scale_t is not None:
                                T.op('dve', lambda e, k=k, oo=oo, po=po, pw=pw, stg=stg: e.tensor_scalar(out=dst[:, k, oo:oo + pw], in0=stg[:, po:po + pw], scalar1=scale_t[:, k:k + 1], scalar2=None, op0=ALU.mult), reads=[sn, 'gn', 'gmm'], writes=[name + '_d'])
                            else:
                                sc = 1.0 if scale_c is None else float(scale_c)
                                T.op('dve', lambda e, k=k, oo=oo, po=po, pw=pw, stg=stg, sc=sc: e.tensor_scalar(out=dst[:, k, oo:oo + pw], in0=stg[:, po:po + pw], scalar1=sc, scalar2=None, op0=ALU.mult), reads=[sn], writes=[name + '_d'])
                        po += pw
                    o += w

        with ExitStack() as es1:
            sb1 = lambda name, shape, dt: es1.enter_context(nc.sbuf_tensor(name, shape, dt))
            xn_l[1] = sb1("xn1", [128, D], BF16)
            Dre = sb1("Dre", [128, 192, 128], BF16)
            Ybuf = sb1("Ybuf", [128, 64, 384], BF16)
            Dim = sb1("Dim", [128, 192, 128], BF16)
            f1 = {k: sb1(k, [128, 256], BF16) for k in ('f1a_p', 'f1b_p', 'f1a_s', 'f1b_s')}
            tw = {k: sb1(k, [128, 512], BF16) for k in ('tcc_p', 'tsp_p', 'tcc_s', 'tsp_s')}
            c2 = sb1("c2", [128, 128], BF16)
            s2 = sb1("s2", [128, 128], BF16)
            hT1_l = [sb1("h1T_p1_%d" % i, [128, 8, 128], BF16) for i in range(2)]
            tAs = [sb1("tA%d" % i, [128, 512], BF16) for i in range(4)]
            tBs = [sb1("tB%d" % i, [128, 512], BF16) for i in range(4)]
            Psb = [sb1("Psb%d" % i, [128, 512], BF16) for i in range(2)]
            Wp = sb1("Wp", [128, 8, 768], BF16)
            with ExitStack() as es1a:
                sb1a = lambda name, shape, dt: es1a.enter_context(nc.sbuf_tensor(name, shape, dt))
                stg = sb1a("stg", [128, 1536], F32)
                stg_h[0] = stg
                n_stg[0] = 1
                Wf1 = sb1a("Wf1", [128, 8, 384], BF16)
                cs96 = sb1a("cs96", [96, 192], BF16)
                WfT = sb1a("WfT", [96, 4, 128], BF16)
                load_const(ident[:], cst['ident'], 128, 'ident')
                load_weight(Wf1, w_in, 8, 384, 'Wf1', scale_t=gn_t)
                T.dma('sp', stg[0:96, 0:192], cst['cs96'][:, :], 'stg0', writes=['stg0'])
                T.op('dve', lambda e: e.tensor_copy(out=cs96[:], in_=stg[0:96, 0:192]), reads=['stg0'], writes=['cs96'])
                for k in range(8):
                    for g in range(4):
                        T.op('pe', lambda e, k=k, g=g: e.transpose(out=pT[0:96, g, :], in_=Wf1[:, k, g * 96:(g + 1) * 96], identity=ident[:]), reads=['Wf1_a', 'Wf1_d', 'ident'], writes=['pT'], signal=(g == 3))
                    T.op('act', lambda e: e.activation(out=WfT[:], in_=pT[0:96, 0:4, :], func=AF.Copy), reads=['pT'], writes=['WfT'])
                    for g in range(4):
                        dst, dn = (pA, 'pA') if g < 2 else (pB, 'pB')
                        o = (g % 2) * 192
                        T.op('pe', lambda e, g=g, dst=dst, o=o: e.matmul(dst[:, o:o + 192], lhsT=WfT[:, g, :], rhs=cs96[:], start=True, stop=True), reads=['WfT', 'cs96'], writes=[dn], signal=(g % 2 == 1))
                    T.op('dve', lambda e, k=k: e.tensor_copy(out=Wp[:, k, 0:384], in_=pA[:, 0:384]), reads=['pA'], writes=['Wp'])
                    T.op('act', lambda e, k=k: e.activation(out=Wp[:, k, 384:768], in_=pB[:, 0:384], func=AF.Copy), reads=['pB'], writes=['Wp'])
                for k in f1:
                    load_const(f1[k][:], cst[k], 256, k)
                for k in tw:
                    load_const(tw[k][:], cst[k].rearrange("p a b c -> p (a b c)"), 512, k)
                load_const(c2[:], cst['c2'], 128, 'c2')
                load_const(s2[:], cst['s2'], 128, 's2')
                T.emit()
            xe_l = [xe[0], xe[1], sb1("xe2", [128, D], F32), sb1("xe3", [128, D], F32)]

            xp_c = xp.rearrange("(r c) d -> c r d", c=64)
            xs_c = xs.rearrange("b (r c) d -> b c r d", c=64)
            for kind in kinds:
                for c in range(ncol):
                    xt = xe_l[c % 4]
                    xslot = 'xe%d' % (c % 4)
                    if kind == 'p':
                        T.dma('sp', xt[:], xp_c[c], xslot + 'a', writes=[xslot + 'a', xslot + 'b'])
                    else:
                        T.dma('sp', xt[0:64, :], xs_c[0, c], xslot + 'a', writes=[xslot + 'a'])
                        T.dma('sp', xt[64:128, :], xs_c[1, c], xslot + 'b', writes=[xslot + 'b'])
                    par = c % 2
                    hT1 = hT1_l[par]
                    h1n = 'hT1_%d' % par
                    bkT, bkTn = (pT, 'pT') if par == 0 else (pT2, 'pS0')
                    bkA, bkAn = (pA, 'pA') if par == 0 else (pS[1], 'pS1')
                    bkB, bkBn = (pB, 'pB') if par == 0 else (pS[2], 'pS2')
                    bkO, bkOn = (pO, 'pO') if par == 0 else (pS[3], 'pS3')
                    rms_to_hT(xt, [xslot + 'a', xslot + 'b'], hT1[:], h1n, par=par, ptv=bkT, ptn=bkTn)
                    for hb, (dst, dn) in enumerate(((bkB, bkBn), (bkO, bkOn))):
                        for k in range(8):
                            T.op('pe', lambda e, k=k, hb=hb, dst=dst, hT1=hT1: e.matmul(dst[:, 0:384], lhsT=hT1[:, k, :], rhs=Wp[:, k, hb * 384:(hb + 1) * 384], start=(k == 0), stop=(k == 7)),
                                 reads=[h1n, 'Wp'], writes=[dn], signal=(k == 7))
                    for half, src, sname in ((0, bkB, bkBn), (1, bkO, bkOn)):
                        v = src[:, 0:384].rearrange("p (g ri c) -> p g ri c", g=2, ri=2)
                        for g in range(2):
                            j0_ = half * 96 + g * 48
                            for ri, (Dt, dn) in enumerate(((Dre, 'Dre'), (Dim, 'Dim'))):
                                if half == 0:
                                    T.op('dve', lambda e, v=v, g=g, j0_=j0_, c=c, ri=ri, Dt=Dt: e.tensor_copy(out=Dt[:, j0_:j0_ + 48, 2 * c:2 * c + 2], in_=v[:, g, ri, :].rearrange("p (j l) -> p j l", l=2)), reads=[sname], writes=[dn + '_w%d_%d' % (half, c % 2)])
                                else:
                                    T.op('act', lambda e, v=v, g=g, j0_=j0_, c=c, ri=ri, Dt=Dt: e.activation(out=Dt[:, j0_:j0_ + 48, 2 * c:2 * c + 2], in_=v[:, g, ri, :].rearrange("p (j l) -> p j l", l=2), func=AF.Copy), reads=[sname], writes=[dn + '_w%d_%d' % (half, c % 2)])
                fa, fb = f1['f1a_' + kind], f1['f1b_' + kind]
                tcc, tsp = tw['tcc_' + kind], tw['tsp_' + kind]
                tcc_v = tcc[:].rearrange("p (a b c) -> p a b c", a=2, b=2)
                tsp_v = tsp[:].rearrange("p (a b c) -> p a b c", a=2, b=2)
                for j4 in range(nj4):
                    p2, p2n = (pS[2], 'pS2') if j4 % 2 == 0 else (pS[3], 'pS3')
                    for jj in range(2):
                        pb = pS[jj]
                        pbn = 'pS%d' % jj
                        bi_ = (j4 % 2) * 2 + jj
                        tA, tB = tAs[bi_], tBs[bi_]
                        tan, tbn = 'tA%d' % bi_, 'tB%d' % bi_
                        for pr in range(2):
                            j = j4 * 4 + jj * 2 + pr
                            T.op('pe', lambda e, j=j, pr=pr, pb=pb, fa=fa: e.matmul(pb[:, pr * 256:(pr + 1) * 256], lhsT=Dre[:, j, :], rhs=fa[:], start=True, stop=False),
                                 reads=['Dre_w0_0', 'Dre_w0_1', 'Dre_w1_0', 'Dre_w1_1', 'f1a_' + kind], writes=[pbn], signal=False)
                            T.op('pe', lambda e, j=j, pr=pr, pb=pb, fb=fb: e.matmul(pb[:, pr * 256:(pr + 1) * 256], lhsT=Dim[:, j, :], rhs=fb[:], start=False, stop=True),
                                 reads=['Dim_w0_0', 'Dim_w0_1', 'Dim_w1_0', 'Dim_w1_1', 'f1b_' + kind], writes=[pbn], signal=(pr == 1))
                        psb = Psb[jj]
                        psn = "Psb%d" % jj
                        T.op('act', lambda e, pb=pb, psb=psb: e.activation(out=psb[:], in_=pb[:], func=AF.Copy), reads=[pbn], writes=[psn])
                        pv = psb[:].rearrange("p (a b c) -> p a b c", a=2, b=2)
                        tAv = tA[:].rearrange("p (a b c) -> p a b c", a=2, b=2)
                        tBv = tB[:].rearrange("p (a b c) -> p a b c", a=2, b=2)
                        T.op('dve', lambda e, psb=psb, tcc=tcc, tA=tA: e.tensor_tensor(out=tA[:], in0=psb[:], in1=tcc[:], op=ALU.mult), reads=[psn, 'tcc_' + kind], writes=[tan])
                        T.op('dve', lambda e, pv=pv, tBv=tBv, tsp_v=tsp_v: e.tensor_tensor(out=tBv[:, :, 0, :], in0=pv[:, :, 1, :], in1=tsp_v[:, :, 0, :], op=ALU.mult), reads=[psn, 'tsp_' + kind], writes=[tbn])
                        T.op('dve', lambda e, pv=pv, tBv=tBv, tsp_v=tsp_v: e.tensor_tensor(out=tBv[:, :, 1, :], in0=pv[:, :, 0, :], in1=tsp_v[:, :, 1, :], op=ALU.mult), reads=[psn, 'tsp_' + kind], writes=[tbn])
                        for pr in range(2):
                            q4 = jj * 2 + pr
                            srcs = [(tAv[:, pr, 0, :], c2, tan), (tBv[:, pr, 0, :], c2, tbn), (tAv[:, pr, 1, :], s2, tan), (tBv[:, pr, 1, :], s2, tbn)]
                            for si, (lh, rh, ln) in enumerate(srcs):
                                T.op('pe', lambda e, lh=lh, rh=rh, q4=q4, p2=p2, si=si: e.matmul(p2[:, q4 * 128:(q4 + 1) * 128], lhsT=lh, rhs=rh[:], start=(si == 0), stop=(si == 3)),
                                     reads=[ln, 'c2', 's2'], writes=[p2n], signal=(pr == 1 and si == 3))
                    T.op('act', lambda e, j4=j4, p2=p2: e.activation(out=Ybuf[:, :, 8 * j4:8 * j4 + 8].rearrange("p k (q l) -> p q k l", q=4),
                                                               in_=p2[:].rearrange("p (q k l) -> p q k l", q=4, k=64), func=AF.Copy), reads=[p2n], writes=['Ybuf'])
                for q8 in range(8 if do_yd else 0):
                    if kind == 'p':
                        T.dma('sp', yd[0:8192, :].rearrange("(k2 k1) ch -> k1 k2 ch", k1=128)[:, q8 * 8:(q8 + 1) * 8, :], Ybuf[:, q8 * 8:(q8 + 1) * 8, :], 'yd', reads=['Ybuf'], writes=['yd%s%d' % (kind, q8)])
                    else:
                        for b in range(2):
                            T.dma('sp', yd[8192 + 4096 * b:8192 + 4096 * (b + 1), :].rearrange("(k2 k1) ch -> k1 k2 ch", k1=64)[:, q8 * 8:(q8 + 1) * 8, :], Ybuf[64 * b:64 * b + 64, q8 * 8:(q8 + 1) * 8, :], 'yd', reads=['Ybuf'], writes=['yd%s%d%d' % (kind, q8, b)])
            T.barrier()
            T.emit()
        if stop == 1:
            return nc
        xn_l[1] = xn_l[0]
        xn_names[1] = 'xn0'

        with ExitStack() as es2:
            sb2 = lambda name, shape, dt: es2.enter_context(nc.sbuf_tensor(name, shape, dt))
            Win = sb2("Win", [128, 8, 5504], BF16)
            Wf = sb2("Wf", [128, 3, D], BF16)
            Wn = sb2("Wn", [128, 3, D], BF16)
            Wc = sb2("Wc", [128, 2, D], BF16)
            Wo = sb2("Wo", [128, 8, D], BF16)
            Tint = sb2("Tint", [128, 6, 10, 64], BF16)
            Tful = sb2("Tful", [128, 6, 14, 64], BF16)
            gf_t = sb2("gf_t", [128, D], F32)
            KmT3 = [sb2("KmT%d" % i, [128, 2, 256], BF16) for i in range(3)]
            Vm3 = [sb2("Vm%d" % i, [128, 2, 4, 65], BF16) for i in range(3)]
            with ExitStack() as es2a:
                sb2a = lambda name, shape, dt: es2a.enter_context(nc.sbuf_tensor(name, shape, dt))
                stg = sb2a("stg2", [128, 2752], F32)
                stg_h[0] = stg
                stg_h[1] = sb2a("stg2b", [128, 2752], F32)
                stg_h[2] = sb2a("stg2c", [128, 2752], F32)
                n_stg[0] = 3
                stg_w[0] = 2752
                Wkv_t = sb2a("Wkv", [128, 8, 512], BF16)
                memT = sb2a("memT", [128, 8, 256], BF16)
                load_weight(Wkv_t, w_kv, 8, 512, 'Wkv', scale_t=gmm_t)
                load_weight(Win, w_in, 8, 5504, 'Win', scale_t=gn_t, col0=384)
                load_weight(Wf, w_f, 3, D, 'Wf', scale_c=0.5)
                load_weight(Wn, w_n, 3, D, 'Wn', scale_c=0.5)
                load_weight(Wc, w_c, 2, D, 'Wc', scale_c=0.5)
                load_weight(Wo, w_o, 8, D, 'Wo', scale_c=0.5)
                T.dma('sp', gf_t[:], gfin[:, :], 'gf', writes=['gf'])
                for nm, tab, nj in (('int', Tint, 10), ('full', Tful, 14)):
                    n = 6 * nj * 64
                    tv = tab[:].rearrange("p h j q -> p (h j q)")
                    o = 0
                    while o < n:
                        w = min(1376, n - o)
                        bi = lw_ctr[0] % n_stg[0]
                        lw_ctr[0] += 1
                        sg_ = stg_h[bi]
                        sn = 'stg%d' % bi
                        T.dma('sp', sg_[:, 0:w], eb['eb_' + nm][:, o:o + w], sn, writes=[sn])
                        T.dma('sp', sg_[:, 1376:1376 + w], eb['em_' + nm][:, o:o + w], sn, writes=[sn])
                        T.op('act', lambda e, w=w, sg_=sg_: e.activation(out=sg_[:, 0:w], in_=sg_[:, 0:w], func=AF.Exp), reads=[sn], writes=[sn])
                        T.op('dve', lambda e, w=w, o=o, tv=tv, sg_=sg_: e.tensor_tensor(out=tv[:, o:o + w], in0=sg_[:, 0:w], in1=sg_[:, 1376:1376 + w], op=ALU.mult), reads=[sn], writes=['T' + nm])
                        o += w
                for mi in range(3):
                    KmT, Vm = KmT3[mi], Vm3[mi]
                    T.op('pool', lambda e, Vm=Vm: e.memset(Vm[:, :, :, 64:65], 1.0), writes=['Vm%d' % mi])
                    for mt in range(2):
                        xt = xe[mt % 2]
                        xslot = 'xe%d' % (mt % 2)
                        T.dma('sp', xt[:], mem[mi, mt * 128:(mt + 1) * 128, :], xslot, writes=[xslot])
                        rms_to_hT(xt, xslot, memT[:, :, mt * 128:(mt + 1) * 128], 'memT', par=mt % 2)
                    for hp in range(2):
                        for k in range(8):
                            T.op('pe', lambda e, hp=hp, k=k: e.matmul(pA[:, 0:256], lhsT=Wkv_t[:, k, hp * 128:(hp + 1) * 128], rhs=memT[:, k, :], start=(k == 0), stop=(k == 7)),
                                 reads=['memT', 'Wkv_a', 'Wkv_d'], writes=['pA'], signal=(k == 7))
                        T.op('act', lambda e, hp=hp, KmT=KmT: e.activation(out=KmT[:, hp, :], in_=pA[:, 0:256], func=AF.Copy), reads=['pA'], writes=['KmT%d' % mi])
                    for ck in range(2):
                        for k in range(8):
                            T.op('pe', lambda e, ck=ck, k=k: e.matmul(pB[:, 0:256], lhsT=memT[:, k, ck * 128:(ck + 1) * 128], rhs=Wkv_t[:, k, 256:512], start=(k == 0), stop=(k == 7)),
                                 reads=['memT', 'Wkv_a', 'Wkv_d'], writes=['pB'], signal=(k == 7))
                        T.op('dve', lambda e, ck=ck, Vm=Vm: e.tensor_copy(out=Vm[:, ck, :, 0:64], in_=pB[:, 0:256].rearrange("p (h d) -> p h d", h=4)), reads=['pB'], writes=['Vm%d' % mi])
                T.barrier()
                T.emit()
            if stop == 2:
                return nc
            NH, NR = 4, 6
            hT = [sb2("hT%d" % i, [128, 8, 128], BF16) for i in range(NH)]
            kT = [sb2("kT%d" % i, [128, 3, 128], BF16) for i in range(NR)]
            Vr = [sb2("Vr%d" % i, [128, 6, 65], BF16) for i in range(NR)]
            Pn = [sb2("Pn%d" % i, [128, 6, 128], BF16) for i in range(5)]
            qT = sb2("qT", [128, 3, 128], BF16)
            qcT = sb2("qcT", [128, 2, 128], BF16)
            Pc = sb2("Pc", [128, 8, 128], BF16)
            th = sb2("th", [128, 384], F32)
            rden = sb2("rden", [128, 6], F32)
            Nn = sb2("Nn", [128, 384], F32)
            act_b = sb2("act_b", [128, 384], BF16)
            BT = [{'f': sb2("FT%d" % i, [128, 3, 128], BF16), 'n': sb2("NT%d" % i, [128, 3, 128], BF16), 'c': sb2("CT%d" % i, [128, 2, 128], BF16)} for i in range(2)]
            th2s = [sb2("th2_%d" % i, [128, 512], F32) for i in range(2)]
            ss2 = sb2("ss2", [128, 1], F32)
            ssp2 = sb2("ssp2", [128, 1], F32)
            rstd2 = sb2("rstd2", [128, 1], F32)
            ybuf = sb2("ybuf", [128, 384], BF16)
            acc = sb2("acc", [128, 512], F32)
            mrg = sb2("mrg", [128, D], BF16)
            mT = sb2("mT", [128, 8, 128], BF16)
            xl = sb2("xl", [128, D], F32)
            for i in range(NR):
                T.op('pool', lambda e, i=i: e.memset(Vr[i][:, :, 64:65], 1.0), writes=['Vr%d' % i])

            def proj_tok(dstbank, bname, hslot, hTt, col, n):
                for k in range(8):
                    T.op('pe', lambda e, k=k: e.matmul(dstbank[:, 0:n], lhsT=hTt[:, k, :], rhs=Win[:, k, col:col + n], start=(k == 0), stop=(k == 7)),
                         reads=[hslot, 'Win'], writes=[bname], signal=(k == 7))

            def proj_feat(dstbank, bname, hslot, hTt, col, nch, ntok=128):
                for ch in range(nch):
                    for k in range(8):
                        T.op('pe', lambda e, k=k, ch=ch: e.matmul(dstbank[:, ch * ntok:(ch + 1) * ntok], lhsT=Win[:, k, col + ch * 128:col + (ch + 1) * 128], rhs=hTt[:, k, :], start=(k == 0), stop=(k == 7)),
                             reads=[hslot, 'Win'], writes=[bname], signal=(k == 7 and ch == nch - 1))

            def silu_gate(bank, bname, n):
                T.op('act', lambda e: e.activation(out=th[:, 0:n], in_=bank[:, 0:n], func=AF.Tanh, scale=0.5), reads=[bname], writes=['th'])
                T.op('dve', lambda e: e.scalar_tensor_tensor(out=th[:, 0:n], in0=th[:, 0:n], scalar=1.0, in1=bank[:, 0:n], op0=ALU.add, op1=ALU.mult), reads=['th', bname], writes=['th'])

            def to_featT(nchunk, dst, dname):
                for ch in range(nchunk):
                    T.op('pe', lambda e, ch=ch: e.transpose(out=pT[:, ch, :], in_=act_b[:, ch * 128:(ch + 1) * 128], identity=ident[:]), reads=['act_b', 'ident'], writes=['pT'], signal=(ch == nchunk - 1))
                T.op('act', lambda e: e.activation(out=dst[:], in_=pT[:, 0:nchunk, :], func=AF.Copy), reads=['pT'], writes=[dname])

            seqs = [(xp, yp, 0, 64, 0), (xs[0], ys[0], 8192, 32, 1), (xs[1], ys[1], 12288, 32, 2)]
            def run_seq(xa, ya, ydo, nt, mi):
                KmT, Vm, vmname, kmname = KmT3[mi], Vm3[mi], 'Vm%d' % mi, 'KmT%d' % mi

                def front(t):
                    xt = xe[t % 2]
                    xslot = 'xe%d' % (t % 2)
                    T.dma('sp', xt[:], xa[t * 128:(t + 1) * 128, :], xslot, writes=[xslot])
                    hs = 'hT%d' % (t % NH)
                    rms_to_hT(xt, xslot, hT[t % NH][:], hs, par=t % 2)
                    yield
                    proj_feat(pTf, 'pT', hs, hT[t % NH], OFF['k'], 3)
                    T.op('act', lambda e, t=t: e.activation(out=kT[t % NR][:].rearrange("p c t -> p (c t)"), in_=pTf[:, 0:384], func=AF.Copy), reads=['pT'], writes=['kT%d' % (t % NR)])
                    yield
                    proj_tok(pTf, 'pT', hs, hT[t % NH], OFF['v'], 384)
                    T.op('dve', lambda e, t=t: e.tensor_copy(out=Vr[t % NR][:, :, 0:64], in_=pTf[:, 0:384].rearrange("p (h d) -> p h d", h=6)), reads=['pT'], writes=['Vr%d' % (t % NR)])
                    yield

                def streamX(t):
                    hs = 'hT%d' % (t % NH)
                    h_ = hT[t % NH]
                    bt = BT[t % 2]
                    btn = 'BT%d' % (t % 2)
                    proj_feat(pA, 'pA', hs, h_, OFF['q'], 3)
                    T.op('act', lambda e: e.activation(out=qT[:].rearrange("p c t -> p (c t)"), in_=pA[:, 0:384], func=AF.Copy), reads=['pA'], writes=['qT'])
                    yield
                    if 2 <= t <= nt - 3:
                        keys = [(tau, Tint, 'Tint', 4 - 2 * (tau - t)) for tau in range(t - 2, t + 3)]
                    elif t < 2:
                        keys = [(tau, Tful, 'Tfull', 6 - 2 * (tau - t)) for tau in range(0, 4)]
                    else:
                        keys = [(tau, Tful, 'Tfull', 6 - 2 * (tau - t)) for tau in range(nt - 4, nt)]
                    for i, (tau, tab, tname, j0) in enumerate(keys):
                        kt = kT[tau % NR]
                        kname = 'kT%d' % (tau % NR)
                        b0, b1 = pS[(2 * i) % 3], pS[(2 * i + 1) % 3]
                        n0, n1 = 'pS%d' % ((2 * i) % 3), 'pS%d' % ((2 * i + 1) % 3)
                        for h in (0, 2, 4, 1, 3, 5):
                            bk = b0 if h % 2 == 0 else b1
                            T.op('pe', lambda e, h=h, bk=bk, kt=kt: e.matmul(bk[:, (h // 2) * 128:(h // 2 + 1) * 128], lhsT=kt[64 * (h % 2):64 * (h % 2) + 64, h // 2, :], rhs=qT[64 * (h % 2):64 * (h % 2) + 64, h // 2, :], start=True, stop=True),
                                 reads=[kname, 'qT'], writes=[n0 if h % 2 == 0 else n1], signal=(h >= 4))
                        P = Pn[i]
                        pname = 'Pn%d' % i
                        T.op('act', lambda e, P=P, b0=b0: e.activation(out=P[:, 0:3, :].rearrange("p h q -> p (h q)"), in_=b0[:, 0:384], func=AF.Exp, scale=0.125), reads=[n0], writes=[pname])
                        T.op('act', lambda e, P=P, b1=b1: e.activation(out=P[:, 3:6, :].rearrange("p h q -> p (h q)"), in_=b1[:, 0:384], func=AF.Exp, scale=0.125), reads=[n1, pname], writes=[pname])
                        T.op('dve', lambda e, P=P, tab=tab, j0=j0: e.tensor_tensor(out=P[:].rearrange("p h (a q) -> p h a q", a=2), in0=P[:].rearrange("p h (a q) -> p h a q", a=2), in1=tab[:, :, j0:j0 + 2, :], op=ALU.mult),
                             reads=[pname, tname], writes=[pname])
                        yield
                    nk = len(keys)
                    for h in range(6):
                        for i, (tau, tab, tname, j0) in enumerate(keys):
                            T.op('pe', lambda e, h=h, i=i, tau=tau: e.matmul(pO[:, h * 65:(h + 1) * 65], lhsT=Pn[i][:, (h % 2) * 3 + h // 2, :], rhs=Vr[tau % NR][:, h, :], start=(i == 0), stop=(i == nk - 1)),
                                 reads=['Pn%d' % i, 'Vr%d' % (tau % NR)], writes=['pO'], signal=(i == nk - 1 and h == 5))
                    proj_tok(pA, 'pA', hs, h_, OFF['gna'], 384)
                    silu_gate(pA, 'pA', 384)
                    yield
                    pOv = pO[:, 0:390].rearrange("p (h d) -> p h d", h=6)
                    T.op('dve', lambda e: e.reciprocal(out=rden[:, 0:6], in_=pOv[:, :, 64]), reads=['pO'], writes=['rden'])
                    T.op('dve', lambda e: e.tensor_tensor(out=Nn[:, 0:384].rearrange("p (h d) -> p h d", h=6), in0=pOv[:, :, 0:64], in1=rden[:, 0:6].unsqueeze(2).broadcast_to([128, 6, 64]), op=ALU.mult), reads=['pO', 'rden'], writes=['Nn'])
                    T.op('dve', lambda e: e.tensor_tensor(out=act_b[:, 0:384], in0=Nn[:, 0:384], in1=th[:, 0:384], op=ALU.mult), reads=['Nn', 'th'], writes=['act_b'])
                    to_featT(3, bt['n'], btn + 'n')
                    yield
                    proj_feat(pA, 'pA', hs, h_, OFF['qca'], 2)
                    T.op('act', lambda e: e.activation(out=qcT[:].rearrange("p c t -> p (c t)"), in_=pA[:, 0:256], func=AF.Copy), reads=['pA'], writes=['qcT'])
                    yield
                    for par in range(2):
                        for h in (par, par + 2):
                            for ck in range(2):
                                sl = (h // 2) * 2 + ck
                                T.op('pe', lambda e, ck=ck, h=h, sl=sl, par=par: e.matmul(pS[par][:, sl * 128:(sl + 1) * 128], lhsT=KmT[64 * (h % 2):64 * (h % 2) + 64, h // 2, ck * 128:(ck + 1) * 128], rhs=qcT[64 * (h % 2):64 * (h % 2) + 64, h // 2, :], start=True, stop=True),
                                     reads=[kmname, 'qcT'], writes=['pS%d' % par], signal=(sl == 3))
                        T.op('act', lambda e, par=par: e.activation(out=Pc[:, par * 4:(par + 1) * 4, :].rearrange("p h q -> p (h q)"), in_=pS[par][:, :], func=AF.Exp, scale=0.125), reads=['pS%d' % par, 'Pc'], writes=['Pc'])
                    yield
                    for h in range(4):
                        for ck in range(2):
                            T.op('pe', lambda e, h=h, ck=ck: e.matmul(pS[2][:, h * 65:(h + 1) * 65], lhsT=Pc[:, (h % 2) * 4 + (h // 2) * 2 + ck, :], rhs=Vm[:, ck, h, :], start=(ck == 0), stop=(ck == 1)),
                                 reads=['Pc', vmname], writes=['pS2'], signal=(ck == 1 and h == 3))
                    proj_tok(pS[0], 'pS0', hs, h_, OFF['gca'], 256)
                    silu_gate(pS[0], 'pS0', 256)
                    yield
                    pBv = pS[2][:, 0:260].rearrange("p (h d) -> p h d", h=4)
                    T.op('dve', lambda e: e.reciprocal(out=rden[:, 0:4], in_=pBv[:, :, 64]), reads=['pS2'], writes=['rden'])
                    T.op('dve', lambda e: e.tensor_tensor(out=Nn[:, 0:256].rearrange("p (h d) -> p h d", h=4), in0=pBv[:, :, 0:64], in1=rden[:, 0:4].unsqueeze(2).broadcast_to([128, 4, 64]), op=ALU.mult), reads=['pS2', 'rden'], writes=['Nn'])
                    T.op('dve', lambda e: e.tensor_tensor(out=act_b[:, 0:256], in0=Nn[:, 0:256], in1=th[:, 0:256], op=ALU.mult), reads=['Nn', 'th'], writes=['act_b'])
                    to_featT(2, bt['c'], btn + 'c')
                    yield
                    T.dma('sp', ybuf[:], yd[ydo + t * 128:ydo + (t + 1) * 128, :], 'ybuf', writes=['ybuf'])
                    proj_tok(pO, 'pO', hs, h_, OFF['gf'], 384)
                    silu_gate(pO, 'pO', 384)
                    T.op('dve', lambda e: e.tensor_tensor(out=act_b[:, 0:384], in0=ybuf[:, :], in1=th[:, 0:384], op=ALU.mult), reads=['ybuf', 'th'], writes=['act_b'])
                    to_featT(3, bt['f'], btn + 'f')
                    yield

                def streamY(t):
                    hs = 'hT%d' % (t % NH)
                    h_ = hT[t % NH]
                    bt = BT[t % 2]
                    btn = 'BT%d' % (t % 2)
                    T.dma('sp', xl[:], xa[t * 128:(t + 1) * 128, :], 'xl', writes=['xl'])
                    for half in range(2):
                        for bi, (bk, W_, nkc) in enumerate((('f', Wf, 3), ('n', Wn, 3), ('c', Wc, 2))):
                            col = GM_OFF + bi * 1024 + half * 512
                            th2 = th2s[(half * 3 + bi) % 2]
                            thn = 'th2_%d' % ((half * 3 + bi) % 2)
                            proj_tok(pB, 'pB', hs, h_, col, 512)
                            T.op('act', lambda e, th2=th2: e.activation(out=th2[:, :], in_=pB[:, :], func=AF.Tanh, scale=0.5), reads=['pB'], writes=[thn])
                            for kc in range(nkc):
                                T.op('pe', lambda e, kc=kc, bk=bk, W_=W_, half=half, bt=bt, nkc=nkc: e.matmul(pS[3][:, :], lhsT=bt[bk][:, kc, :], rhs=W_[:, kc, half * 512:(half + 1) * 512], start=(kc == 0), stop=(kc == nkc - 1)),
                                     reads=[btn + bk, 'W' + bk], writes=['pS3'], signal=(kc == nkc - 1))
                            yield
                            if bi == 0:
                                T.op('dve', lambda e, th2=th2: e.scalar_tensor_tensor(out=acc[:, :], in0=th2[:, :], scalar=1.0, in1=pS[3][:, :], op0=ALU.add, op1=ALU.mult), reads=[thn, 'pS3'], writes=['acc'])
                            else:
                                T.op('dve', lambda e, th2=th2: e.scalar_tensor_tensor(out=th2[:, :], in0=th2[:, :], scalar=1.0, in1=pS[3][:, :], op0=ALU.add, op1=ALU.mult), reads=[thn, 'pS3'], writes=[thn])
                                if bi == 1:
                                    T.op('dve', lambda e, th2=th2: e.tensor_tensor(out=acc[:, :], in0=acc[:, :], in1=th2[:, :], op=ALU.add), reads=['acc', thn], writes=['acc'])
                                else:
                                    T.op('dve', lambda e, half=half, th2=th2: e.tensor_tensor(out=mrg[:, half * 512:(half + 1) * 512], in0=acc[:, :], in1=th2[:, :], op=ALU.add), reads=['acc', thn], writes=['mrg'])
                    yield
                    for k in range(8):
                        T.op('pe', lambda e, k=k: e.transpose(out=pT[:, k, :], in_=mrg[:, k * 128:(k + 1) * 128], identity=ident[:]), reads=['mrg', 'ident'], writes=['pT'], signal=(k == 7))
                    T.op('act', lambda e: e.activation(out=mT[:].rearrange("p k t -> p (k t)"), in_=pT[:].rearrange("p k t -> p (k t)"), func=AF.Copy), reads=['pT'], writes=['mT'])
                    yield
                    for half in range(2):
                        ob, obn = (pB, 'pB') if half == 0 else (pS[3], 'pS3')
                        for k in range(8):
                            T.op('pe', lambda e, k=k, half=half, ob=ob: e.matmul(ob[:, :], lhsT=mT[:, k, :], rhs=Wo[:, k, half * 512:(half + 1) * 512], start=(k == 0), stop=(k == 7)),
                                 reads=['mT', 'Wo'], writes=[obn], signal=(k == 7))
                    for half in range(2):
                        ob, obn = (pB, 'pB') if half == 0 else (pS[3], 'pS3')
                        T.op('dve', lambda e, half=half, ob=ob: e.tensor_tensor(out=xl[:, half * 512:(half + 1) * 512], in0=xl[:, half * 512:(half + 1) * 512], in1=ob[:, :], op=ALU.add), reads=['xl', obn], writes=['xl'])
                    if True:
                        yield
                    T.op('act', lambda e: e.activation(out=mrg[:], in_=xl[:], func=AF.Square, scale=1.0 / 32, accum_out=ss2[:]), reads=['xl'], writes=['mrg', 'ss2'])
                    T.op('dve', lambda e: e.tensor_scalar(out=ssp2[:], in0=ss2[:], scalar1=EPS, scalar2=None, op0=ALU.add), reads=['ss2'], writes=['ssp2'])
                    T.op('pool', lambda e: e.tensor_tensor(out=rstd2[:], in0=ssp2[:], in1=mh[:], op=ALU.pow), reads=['ssp2', 'mh'], writes=['rstd2'])
                    T.op('dve', lambda e: e.scalar_tensor_tensor(out=xl[:], in0=xl[:], scalar=rstd2[:, 0:1], in1=gf_t[:], op0=ALU.mult, op1=ALU.mult), reads=['xl', 'rstd2', 'gf'], writes=['xl'])
                    T.dma('pool', ya[t * 128:(t + 1) * 128, :], xl[:], 'out', reads=['xl'])
                    yield

                def interleave(specs):
                    live = [[g, d] for g, d in specs if g is not None]
                    rnd = 0
                    while live:
                        for it in list(live):
                            if it[1] > rnd:
                                continue
                            try:
                                next(it[0])
                            except StopIteration:
                                live.remove(it)
                        rnd += 1

                def run(g):
                    for _ in g:
                        pass

                for tt in range(3):
                    run(front(tt))
                for t in range(nt):
                    if t + 3 < nt:
                        run(front(t + 3))
                    run(streamX(t))
                    run(streamY(t))

            for sq_ in seqs:
                if ntl is not None:
                    sq_ = (sq_[0], sq_[1], sq_[2], ntl, sq_[4])
                run_seq(*sq_)
            T.barrier()
            T.emit()
    nc._trk_stats = T.stats
    return nc


_NC_CACHE = {}


def kernel(x_prompt, x_sample, mem_prompt, mem_sample, g_norm, w_in, na_rpb, g_mem, w_mem_kv,
           w_f_out, w_na_out, w_ca_out, w_out, g_final):
    f = lambda a: np.ascontiguousarray(np.asarray(a, dtype=np.float32))
    x_prompt, x_sample, mem_prompt, mem_sample = f(x_prompt), f(x_sample), f(mem_prompt), f(mem_sample)
    if 'nc' not in _NC_CACHE:
        _NC_CACHE['nc'] = build_nc()
    nc = _NC_CACHE['nc']
    shared = {
        "gn": f(np.asarray(g_norm)[0].reshape(8, 128).T),
        "gmm": f(np.asarray(g_mem)[0].reshape(8, 128).T),
        "gfin": f(np.broadcast_to(np.asarray(g_final)[None, :], (128, D))),
        "w_in": f(np.asarray(w_in)[0]), "w_kv": f(np.asarray(w_mem_kv)[0]), "w_f": f(np.asarray(w_f_out)[0]),
        "w_n": f(np.asarray(w_na_out)[0]), "w_c": f(np.asarray(w_ca_out)[0]), "w_o": f(np.asarray(w_out)[0]),
    }
    for k, v in _consts().items():
        shared["c_" + k] = f(v)
    shared.update(_build_etabs(np.asarray(na_rpb, dtype=np.float32)[0]))
    in_maps = []
    for i in range(NCORES):
        m = dict(shared)
        m["xp"] = x_prompt[i]
        m["xs"] = x_sample[2 * i:2 * i + 2]
        m["mem"] = f(np.concatenate([mem_prompt[i:i + 1], mem_sample[2 * i:2 * i + 2]], 0))
        in_maps.append(m)
    res = run_bass_kernel_spmd(nc, in_maps, core_ids=list(range(NCORES)))
    y_p = np.stack([np.asarray(r["yp"], dtype=np.float32) for r in res.results], 0)
    y_s = np.concatenate([np.asarray(r["ys"], dtype=np.float32) for r in res.results], 0)
    return (y_p, y_s)
```

```python
import numpy as np
from contextlib import ExitStack
import concourse.bass as bass
import concourse.mybir as mybir
from concourse.bass_utils import run_bass_kernel_spmd

F32 = mybir.dt.float32
BF16 = mybir.dt.bfloat16
AF = mybir.ActivationFunctionType
ALU = mybir.AluOpType

D = 1024
EPS = 1e-6
NCORES = 8
GM_OFF = 2432
OFF = dict(gf=0, q=384, k=768, v=1152, gna=1536, qca=1920, gca=2176)


USE_RANK = True
PSUM_SLOTS = {'pT', 'pA', 'pB', 'pS0', 'pS1', 'pS2', 'pS3', 'pO'}


class Slot:
    def __init__(self, name):
        self.name = name
        self.w = []
        self.r = []


class _Probe:
    def __init__(self):
        self.rec = None

    def __getattr__(self, name):
        def f(*a, **k):
            self.rec = (name, a, k)
            return self
        return f


def _free_size(ap):
    n = 1
    for d in ap.shape[1:]:
        n *= int(d)
    return n


def _auto_cost(eng, fn):
    p = _Probe()
    try:
        fn(p)
        name, a, k = p.rec
        out = k.get('out', a[0] if a else None)
        n = _free_size(out)
        if eng == 'pe':
            if name == 'transpose':
                return 0.085
            return max(0.057, n / 2400.0 + 0.012)
        if eng == 'act':
            return 0.17 + n / 1200.0 + (0.1 if 'accum_out' in k else 0.0)
        if eng == 'dve':
            return 0.1 + n / 1400.0
        if eng == 'pool':
            return 0.25 + n / 450.0
    except Exception:
        pass
    return None


class Trk:
    ENG = ['pe', 'act', 'dve', 'pool', 'sp']
    COST = {'pe': 0.12, 'act': 0.6, 'dve': 0.5, 'pool': 0.9, 'sp': 2.5}
    HOP = 0.2
    HOP_SAME = 0.08
    WINDOW = 120

    def __init__(self, nc, es):
        self.nc = nc
        self.es = es
        self.psem = {e: es.enter_context(nc.semaphore('prog_' + e)) for e in ['pe', 'act', 'dve', 'pool']}
        self.cnt = {e: 0 for e in self.psem}
        self.dsem = {}
        self.dcnt = {}
        self.slots = {}
        self.units = []
        self.pend = {}
        self.waited = {e: {} for e in self.ENG}
        self.reorder = True
        self.use_rank = USE_RANK
        self.diag = None

    def S(self, name):
        if name not in self.slots:
            self.slots[name] = Slot(name)
        return self.slots[name]

    def _finish_unit(self, eng, fns, reads, writes, cost, semname=None):
        uid = len(self.units)
        deps = set()
        for s in reads:
            deps.update(s.w)
            if s.name in PSUM_SLOTS:
                deps.update(s.r)
        for s in writes:
            deps.update(s.w)
            deps.update(s.r)
        deps.discard(uid)
        self.units.append(dict(eng=eng, fns=fns, deps=deps, cost=cost, sem=semname, rn=[x.name for x in reads], wn=[x.name for x in writes]))
        for s in reads:
            s.r.append(uid)
        for s in writes:
            s.w = [uid]
            s.r = []
        return uid

    def op(self, eng, fn, reads=(), writes=(), signal=True, cost=None):
        reads = [self.S(n) for n in reads]
        writes = [self.S(n) for n in writes]
        p = self.pend.setdefault(eng, dict(fns=[], reads=[], writes=[], cost=0.0))
        p['fns'].append(fn)
        p['reads'] += [x for x in reads if x not in p['reads']]
        p['writes'] += [x for x in writes if x not in p['writes']]
        if cost is None:
            cost = _auto_cost(eng, fn)
        p['cost'] += (self.COST[eng] if cost is None else cost)
        if signal:
            del self.pend[eng]
            self._finish_unit(eng, p['fns'], p['reads'], p['writes'], p['cost'])

    def dma(self, eng, out, in_, semname, reads=(), writes=(), cost=None):
        reads = [self.S(n) for n in reads]
        writes = [self.S(n) for n in writes]
        if semname not in self.dsem:
            self.dsem[semname] = self.es.enter_context(self.nc.semaphore('d_' + semname))
            self.dcnt[semname] = 0
        fn = lambda e, out=out, in_=in_: e.dma_start(out=out, in_=in_)
        if cost is None:
            try:
                cost = 2.0 + out.nbytes() / 250e3
            except Exception:
                cost = self.COST['sp']
        self._finish_unit(eng, [fn], reads, writes, cost, semname=semname)

    def barrier(self):
        pass

    def _schedule(self):
        U = self.units
        n = len(U)
        order = {e: [] for e in self.ENG}
        if not self.reorder:
            for i, u in enumerate(U):
                order[u['eng']].append(i)
            return order
        ndep = [len(u['deps']) for u in U]
        users = [[] for _ in range(n)]
        for i, u in enumerate(U):
            for d in u['deps']:
                users[d].append(i)
        ready = [0.0 if ndep[i] == 0 else None for i in range(n)]
        fin = [None] * n
        queue = {e: [] for e in self.ENG}
        for i, u in enumerate(U):
            queue[u['eng']].append(i)
        head = {e: 0 for e in self.ENG}
        done = [False] * n
        free_at = {e: 0.0 for e in self.ENG}
        left = n
        W = self.WINDOW
        rank = [0.0] * n
        if self.use_rank:
            for i in range(n - 1, -1, -1):
                r = 0.0
                for k in users[i]:
                    v = rank[k] + (self.HOP_SAME if U[k]['eng'] == U[i]['eng'] else self.HOP)
                    if v > r:
                        r = v
                rank[i] = r + U[i]['cost']
        while left:
            best = None
            for e in self.ENG:
                q = queue[e]
                h = head[e]
                while h < len(q) and done[q[h]]:
                    h += 1
                head[e] = h
                fa = free_at[e]
                cnt = 0
                j = h
                cand = None
                while j < len(q) and cnt < W:
                    i = q[j]
                    j += 1
                    if done[i]:
                        continue
                    cnt += 1
                    r = ready[i]
                    if r is None:
                        continue
                    st = r if r > fa else fa
                    key = (st, -rank[i], i)
                    if cand is None or key < cand:
                        cand = key
                    if not self.use_rank and r <= fa:
                        break
                if cand is not None:
                    key = (cand[0], cand[1], cand[2], e)
                    if best is None or key[:3] < best[:3]:
                        best = key
            best = (best[0], best[2], best[3])
            st, i, e = best
            if self.diag is not None and st > free_at[e] + 0.2 and U[i]['deps']:
                d = max(U[i]['deps'], key=lambda d: fin[d])
                key = (e, U[d]['eng'], tuple(U[d]['wn'][:2]), tuple(U[i]['wn'][:1]))
                self.diag[key] = self.diag.get(key, 0.0) + (st - free_at[e])
            f = st + U[i]['cost']
            fin[i] = f
            free_at[e] = (st + 0.15) if U[i]['sem'] is not None else f
            done[i] = True
            order[e].append(i)
            left -= 1
            for k in users[i]:
                ndep[k] -= 1
                if ndep[k] == 0:
                    ek = U[k]['eng']
                    ready[k] = max(fin[d] + (self.HOP_SAME if U[d]['eng'] == ek else self.HOP) for d in U[k]['deps'])
        self.sim_time = max(free_at.values())
        return order

    def emit(self):
        nc = self.nc
        assert not self.pend, self.pend.keys()
        U = self.units
        order = self._schedule()
        self.stats = getattr(self, 'stats', []) + [(len(U), {e: len(order[e]) for e in self.ENG}, getattr(self, 'sim_time', None))]
        tok = [None] * len(U)
        for e in self.ENG:
            for i in order[e]:
                u = U[i]
                if u['sem'] is not None:
                    self.dcnt[u['sem']] += 16
                    tok[i] = (u['sem'], self.dcnt[u['sem']])
                else:
                    self.cnt[e] += 1
                    tok[i] = (e, self.cnt[e])
        lists = {e: [] for e in self.ENG}
        for e in self.ENG:
            L = lists[e]
            wd = self.waited[e]
            for i in order[e]:
                u = U[i]
                need = {}
                for d in u['deps']:
                    k, v = tok[d]
                    if v > need.get(k, 0):
                        need[k] = v
                for k, v in need.items():
                    if wd.get(k, 0) >= v:
                        continue
                    wd[k] = v
                    sem = self.psem[k] if k in self.psem else self.dsem[k]
                    L.append(lambda en, sem=sem, v=v: en.wait_ge(sem, v))
                fns = u['fns']
                for fn in fns[:-1]:
                    L.append(fn)
                if u['sem'] is not None:
                    sem = self.dsem[u['sem']]
                    L.append(lambda en, fn=fns[-1], sem=sem: fn(en).then_inc(sem, 16))
                else:
                    sem = self.psem[e]
                    L.append(lambda en, fn=fns[-1], sem=sem: fn(en).then_inc(sem, 1))
            for k, v in list(self.cnt.items()) + list(self.dcnt.items()):
                if v > 0 and wd.get(k, 0) < v:
                    wd[k] = v
                    sem = self.psem[k] if k in self.psem else self.dsem[k]
                    L.append(lambda en, sem=sem, v=v: en.wait_ge(sem, v))
        with nc.Block() as block:
            @block.tensor
            def _(en):
                for f in lists['pe']:
                    f(en)

            @block.scalar
            def _(en):
                for f in lists['act']:
                    f(en)

            @block.vector
            def _(en):
                for f in lists['dve']:
                    f(en)

            @block.gpsimd
            def _(en):
                for f in lists['pool']:
                    f(en)

            @block.sync
            def _(en):
                for f in lists['sp']:
                    f(en)
        self.units = []
        for s in self.slots.values():
            s.w = []
            s.r = []


def _consts():
    c = {}
    a = np.arange(96)
    ang = 2 * np.pi * np.outer(a, a) / 96
    c['cs96'] = np.concatenate([np.cos(ang), -np.sin(ang)], 1).astype(np.float32)
    r = np.arange(128)
    ang = 2 * np.pi * np.outer(r, r) / 128
    sc = 1.0 / np.sqrt(96.0 * 8192.0)
    C, S_ = np.cos(ang) * sc, np.sin(ang) * sc
    c['f1a_p'] = np.concatenate([C, -S_], 1).astype(np.float32)
    c['f1b_p'] = np.concatenate([S_, C], 1).astype(np.float32)
    r64 = np.arange(64)
    ang = 2 * np.pi * np.outer(r64, r64) / 64
    sc = 1.0 / np.sqrt(96.0 * 4096.0)
    C64, S64 = np.cos(ang) * sc, np.sin(ang) * sc
    Cb = np.zeros((128, 128)); Sb = np.zeros((128, 128))
    for b in range(2):
        Cb[64 * b:64 * b + 64, 64 * b:64 * b + 64] = C64
        Sb[64 * b:64 * b + 64, 64 * b:64 * b + 64] = S64
    c['f1a_s'] = np.concatenate([Cb, -Sb], 1).astype(np.float32)
    c['f1b_s'] = np.concatenate([Sb, Cb], 1).astype(np.float32)
    m = np.arange(128)
    cc = m // 2
    th_p = 2 * np.pi * np.outer(cc, np.arange(128)) / 8192.0
    th_s = 2 * np.pi * np.outer(cc, np.arange(128) % 64) / 4096.0
    for nm, th in (('p', th_p), ('s', th_s)):
        tc = np.cos(th); ts = np.sin(th)
        tcc = np.stack([np.stack([tc, tc], 1)] * 2, 1)
        tsp = np.stack([np.stack([ts, -ts], 1)] * 2, 1)
        c['tcc_' + nm] = tcc.astype(np.float32)
        c['tsp_' + nm] = tsp.astype(np.float32)
    ang = 2 * np.pi * np.outer(r64, r64) / 64
    C2 = np.zeros((64, 2, 64, 2)); S2 = np.zeros((64, 2, 64, 2))
    for l in range(2):
        C2[:, l, :, l] = np.cos(ang)
        S2[:, l, :, l] = np.sin(ang)
    c['c2'] = C2.reshape(128, 128).astype(np.float32)
    c['s2'] = S2.reshape(128, 128).astype(np.float32)
    c['ident'] = np.eye(128, dtype=np.float32)
    return c


def _etab_index():
    kc = np.arange(64)[:, None]
    qc = np.arange(64)[None, :]
    cs = np.clip(qc - 8, 0, 48)
    colvalid = ((kc - cs) >= 0) & ((kc - cs) < 16)
    dc = np.clip(kc - qc + 15, 0, 30)
    out = {}
    for nm, nj, c0, lo, hi in (('int', 10, 4, -4, 3), ('full', 14, 6, -7, 7)):
        dri = np.zeros((2, nj), np.int64)
        rv = np.zeros((2, nj), bool)
        for kh in range(2):
            for j in range(nj):
                dr = kh - j + c0
                rv[kh, j] = (lo <= dr <= hi)
                dri[kh, j] = np.clip(dr, -7, 7) + 7
        mask = (rv[:, None, :, None] & colvalid[None, :, None, :])
        out[nm] = (dri, mask.astype(np.float32), nj)
    return out, dc


def _build_etabs(rpb):
    idx, dc = _etab_index()
    res = {}
    for nm in ('int', 'full'):
        dri, mask, nj = idx[nm]
        g = rpb[:, dri[:, :, None, None], dc[None, None, :, :]]
        g = np.transpose(g, (1, 3, 0, 2, 4))
        g = g[:, :, [0, 2, 4, 1, 3, 5]]
        res['eb_' + nm] = np.ascontiguousarray(g.reshape(128, 6 * nj * 64)).astype(np.float32)
        m = np.broadcast_to(mask[:, :, None, :, :], (2, 64, 6, nj, 64))
        res['em_' + nm] = np.ascontiguousarray(m.reshape(128, 6 * nj * 64)).astype(np.float32)
    return res


def build_nc(stop=0, ntl=None, ncol=64, nj4=48, do_yd=True, kinds=('p', 's'), lvl=9, lvl2=9):
    nc = bass.Bass("TRN2", target_bir_lowering=False)
    dt_in = lambda name, shape: nc.dram_tensor(name, list(shape), F32, kind="ExternalInput").ap()
    xp = dt_in("xp", [8192, D])
    xs = dt_in("xs", [2, 4096, D])
    mem = dt_in("mem", [3, 256, D])
    gn = dt_in("gn", [128, 8])
    gmm = dt_in("gmm", [128, 8])
    gfin = dt_in("gfin", [128, D])
    w_in = dt_in("w_in", [D, 5888])
    w_kv = dt_in("w_kv", [D, 512])
    w_f = dt_in("w_f", [384, D])
    w_n = dt_in("w_n", [384, D])
    w_c = dt_in("w_c", [256, D])
    w_o = dt_in("w_o", [D, D])
    cst = {k: dt_in("c_" + k, v.shape) for k, v in _consts().items()}
    eb = {}
    for nm, nj in (('int', 10), ('full', 14)):
        eb['eb_' + nm] = dt_in("eb_" + nm, [128, 6 * nj * 64])
        eb['em_' + nm] = dt_in("em_" + nm, [128, 6 * nj * 64])
    yp = nc.dram_tensor("yp", [8192, D], F32, kind="ExternalOutput").ap()
    ys = nc.dram_tensor("ys", [2, 4096, D], F32, kind="ExternalOutput").ap()
    yd = nc.dram_tensor("yscr", [16384, 384], BF16).ap()

    with ExitStack() as es:
        T = Trk(nc, es)
        sb = lambda name, shape, dt: es.enter_context(nc.sbuf_tensor(name, shape, dt))
        ps = lambda name, shape, dt: es.enter_context(nc.psum_tensor(name, shape, dt))
        pTf = ps("pT", [128, 512], F32)
        pT = pTf[:].bitcast(BF16).rearrange("p (k t) -> p k t", k=8)
        pA = ps("pA", [128, 512], F32)
        pB = ps("pB", [128, 512], F32)
        pS = [ps("pS%d" % i, [128, 512], F32) for i in range(4)]
        pO = ps("pO", [128, 512], F32)
        ident = sb("ident", [128, 128], BF16)
        mh = sb("mh", [128, 1], F32)
        gn_t = sb("gn_t", [128, 8], F32)
        gmm_t = sb("gmm_t", [128, 8], F32)
        ss_l = [sb("ss%d" % i, [128, 1], F32) for i in range(2)]
        ssp_l = [sb("ssp%d" % i, [128, 1], F32) for i in range(2)]
        rstd_l = [sb("rstd%d" % i, [128, 1], F32) for i in range(2)]
        xn_l = [sb("xn0", [128, D], BF16), None]
        xn_names = ['xn0', 'xn1']
        xe = [sb("xe%d" % i, [128, D], F32) for i in range(2)]

        stg_h = [None, None, None, None]
        n_stg = [2]
        stg_w = [1536]

        def load_const(dst, src_ap, shape2, name):
            stg = stg_h[0]
            n = shape2
            o = 0
            while o < n:
                w = min(1536, n - o)
                T.dma('sp', stg[:, 0:w], src_ap[:, o:o + w], 'stg0', writes=['stg0'])
                T.op('dve', lambda e, o=o, w=w: e.tensor_copy(out=dst[:, o:o + w], in_=stg[:, 0:w]), reads=['stg0'], writes=[name])
                o += w

        T.op('dve', lambda e: e.memset(mh[:], -0.5), writes=['mh'])
        T.dma('sp', gn_t[:], gn[:, :], 'gn', writes=['gn'])
        T.dma('sp', gmm_t[:], gmm[:, :], 'gmm', writes=['gmm'])

        pT2 = pS[0][:].bitcast(BF16).rearrange("p (k t) -> p k t", k=8)

        def rms_to_hT(xt, xslot, hT_dst, hslot, par=0, ptv=None, ptn='pT', copy_eng='dve'):
            if ptv is None:
                ptv = pT
            xs_ = xslot if isinstance(xslot, list) else [xslot]
            xn, ss, ssp, rstd = xn_l[par], ss_l[par], ssp_l[par], rstd_l[par]
            xnn, ssn, sspn, rsn = xn_names[par], 'ss%d' % par, 'ssp%d' % par, 'rstd%d' % par
            T.op('act', lambda e: e.activation(out=xn[:], in_=xt[:], func=AF.Square, scale=1.0 / 32, accum_out=ss[:]), reads=xs_, writes=[xnn, ssn])
            T.op('dve', lambda e: e.tensor_scalar(out=ssp[:], in0=ss[:], scalar1=EPS, scalar2=None, op0=ALU.add), reads=[ssn], writes=[sspn])
            T.op('pool', lambda e: e.tensor_tensor(out=rstd[:], in0=ssp[:], in1=mh[:], op=ALU.pow), reads=[sspn, 'mh'], writes=[rsn])
            T.op('dve', lambda e: e.tensor_scalar(out=xn[:], in0=xt[:], scalar1=rstd[:, 0:1], scalar2=None, op0=ALU.mult), reads=xs_ + [rsn], writes=[xnn])
            for k in range(8):
                T.op('pe', lambda e, k=k: e.transpose(out=ptv[:, k, :], in_=xn[:, k * 128:(k + 1) * 128], identity=ident[:]), reads=[xnn, 'ident'], writes=[ptn], signal=(k == 7))
            if copy_eng == 'dve':
                T.op('dve', lambda e: e.tensor_copy(out=hT_dst, in_=ptv[:]), reads=[ptn], writes=[hslot])
            else:
                T.op('act', lambda e: e.activation(out=hT_dst, in_=ptv[:], func=AF.Copy), reads=[ptn], writes=[hslot])

        lw_ctr = [0]

        def load_weight(dst, src, nk, ncols, name, scale_t=None, scale_c=None, col0=0):
            cw = stg_w[0]
            for k in range(nk):
                o = 0
                while o < ncols:
                    w = min(cw, ncols - o)
                    bi = lw_ctr[0] % n_stg[0]
                    lw_ctr[0] += 1
                    stg = stg_h[bi]
                    sn = 'stg%d' % bi
                    T.dma('sp', stg[:, 0:w], src[k * 128:(k + 1) * 128, col0 + o:col0 + o + w], sn, writes=[sn])
                    po = 0
                    pi = 0
                    while po < w:
                        pw = min(1408, w - po)
                        eng = 'act' if (pi + bi) % 2 == 0 else 'dve'
                        pi += 1
                        oo = o + po
                        if eng == 'act':
                            if scale_t is not None:
                                T.op('act', lambda e, k=k, oo=oo, po=po, pw=pw, stg=stg: e.activation(out=dst[:, k, oo:oo + pw], in_=stg[:, po:po + pw], func=AF.Copy, scale=scale_t[:, k:k + 1]), reads=[sn, 'gn', 'gmm'], writes=[name + '_a'])
                            else:
                                sc = 1.0 if scale_c is None else float(scale_c)
                                T.op('act', lambda e, k=k, oo=oo, po=po, pw=pw, stg=stg, sc=sc: e.activation(out=dst[:, k, oo:oo + pw], in_=stg[:, po:po + pw], func=AF.Copy, scale=sc), reads=[sn], writes=[name + '_a'])
                        else:
                            if scale_t is not None:
                                T.op('dve', lambda e, k=k, oo=oo, po=po, pw=pw, stg=stg: e.tensor_scalar(out=dst[:, k, oo:oo + pw], in0=stg[:, po:po + pw], scalar1=scale_t[:, k:k + 1], scalar2=None, op0=ALU.mult), reads=[sn, 'gn', 'gmm'], writes=[name + '_d'])
                            else:
                                sc = 1.0 if scale_c is None else float(scale_c)
                                T.op('dve', lambda e, k=k, oo=oo, po=po, pw=pw, stg=stg, sc=sc: e.tensor_scalar(out=dst[:, k, oo:oo + pw], in0=stg[:, po:po + pw], scalar1=sc, scalar2=None, op0=ALU.mult), reads=[sn], writes=[name + '_d'])
                        po += pw
                    o += w

        with ExitStack() as es1:
            sb1 = lambda name, shape, dt: es1.enter_context(nc.sbuf_tensor(name, shape, dt))
            xn_l[1] = sb1("xn1", [128, D], BF16)
            Dre = sb1("Dre", [128, 192, 128], BF16)
            Ybuf = sb1("Ybuf", [128, 64, 384], BF16)
            Dim = sb1("Dim", [128, 192, 128], BF16)
            f1 = {k: sb1(k, [128, 256], BF16) for k in ('f1a_p', 'f1b_p', 'f1a_s', 'f1b_s')}
            tw = {k: sb1(k, [128, 512], BF16) for k in ('tcc_p', 'tsp_p', 'tcc_s', 'tsp_s')}
            c2 = sb1("c2", [128, 128], BF16)
            s2 = sb1("s2", [128, 128], BF16)
            hT1_l = [sb1("h1T_p1_%d" % i, [128, 8, 128], BF16) for i in range(2)]
            tAs = [sb1("tA%d" % i, [128, 512], BF16) for i in range(4)]
            tBs = [sb1("tB%d" % i, [128, 512], BF16) for i in range(4)]
            Psb = [sb1("Psb%d" % i, [128, 512], BF16) for i in range(2)]
            Wp = sb1("Wp", [128, 8, 768], BF16)
            with ExitStack() as es1a:
                sb1a = lambda name, shape, dt: es1a.enter_context(nc.sbuf_tensor(name, shape, dt))
                stg = sb1a("stg", [128, 1536], F32)
                stg_h[0] = stg
                n_stg[0] = 1
                Wf1 = sb1a("Wf1", [128, 8, 384], BF16)
                cs96 = sb1a("cs96", [96, 192], BF16)
                WfT = sb1a("WfT", [96, 4, 128], BF16)
                load_const(ident[:], cst['ident'], 128, 'ident')
                load_weight(Wf1, w_in, 8, 384, 'Wf1', scale_t=gn_t)
                T.dma('sp', stg[0:96, 0:192], cst['cs96'][:, :], 'stg0', writes=['stg0'])
                T.op('dve', lambda e: e.tensor_copy(out=cs96[:], in_=stg[0:96, 0:192]), reads=['stg0'], writes=['cs96'])
                for k in range(8):
                    for g in range(4):
                        T.op('pe', lambda e, k=k, g=g: e.transpose(out=pT[0:96, g, :], in_=Wf1[:, k, g * 96:(g + 1) * 96], identity=ident[:]), reads=['Wf1_a', 'Wf1_d', 'ident'], writes=['pT'], signal=(g == 3))
                    T.op('act', lambda e: e.activation(out=WfT[:], in_=pT[0:96, 0:4, :], func=AF.Copy), reads=['pT'], writes=['WfT'])
                    for g in range(4):
                        dst, dn = (pA, 'pA') if g < 2 else (pB, 'pB')
                        o = (g % 2) * 192
                        T.op('pe', lambda e, g=g, dst=dst, o=o: e.matmul(dst[:, o:o + 192], lhsT=WfT[:, g, :], rhs=cs96[:], start=True, stop=True), reads=['WfT', 'cs96'], writes=[dn], signal=(g % 2 == 1))
                    T.op('dve', lambda e, k=k: e.tensor_copy(out=Wp[:, k, 0:384], in_=pA[:, 0:384]), reads=['pA'], writes=['Wp'])
                    T.op('act', lambda e, k=k: e.activation(out=Wp[:, k, 384:768], in_=pB[:, 0:384], func=AF.Copy), reads=['pB'], writes=['Wp'])
                for k in f1:
                    T.dma('pool', f1[k][:], cst[k][:, :], 'c_' + k, writes=[k])
                for k in tw:
                    T.dma('pool', tw[k][:], cst[k].rearrange("p a b c -> p (a b c)"), 'c_' + k, writes=[k])
                T.dma('pool', c2[:], cst['c2'][:, :], 'c_c2', writes=['c2'])
                T.dma('pool', s2[:], cst['s2'][:, :], 'c_s2', writes=['s2'])
                T.emit()
            xe_l = [xe[0], xe[1], sb1("xe2", [128, D], F32), sb1("xe3", [128, D], F32)]

            xp_c = xp.rearrange("(r c) d -> c r d", c=64)
            xs_f = xs.rearrange("b (r c) d -> c (b r) d", c=64)
            for kind in kinds:
                for c in range(ncol):
                    xt = xe_l[c % 4]
                    xslot = 'xe%d' % (c % 4)
                    if kind == 'p':
                        T.dma('sp', xt[:], xp_c[c], xslot + 'a', writes=[xslot + 'a', xslot + 'b'])
                    else:
                        T.dma('sp', xt[:], xs_f[c], xslot + 'a', writes=[xslot + 'a', xslot + 'b'])
                    par = c % 2
                    hT1 = hT1_l[par]
                    h1n = 'hT1_%d' % par
                    bkT, bkTn = (pT, 'pT') if par == 0 else (pT2, 'pS0')
                    bkA, bkAn = (pA, 'pA') if par == 0 else (pS[1], 'pS1')
                    bkB, bkBn = (pB, 'pB') if par == 0 else (pS[2], 'pS2')
                    bkO, bkOn = (pO, 'pO') if par == 0 else (pS[3], 'pS3')
                    rms_to_hT(xt, [xslot + 'a', xslot + 'b'], hT1[:], h1n, par=par, ptv=bkT, ptn=bkTn)
                    for hb, (dst, dn) in enumerate(((bkB, bkBn), (bkO, bkOn))):
                        for k in range(8):
                            T.op('pe', lambda e, k=k, hb=hb, dst=dst, hT1=hT1: e.matmul(dst[:, 0:384], lhsT=hT1[:, k, :], rhs=Wp[:, k, hb * 384:(hb + 1) * 384], start=(k == 0), stop=(k == 7)),
                                 reads=[h1n, 'Wp'], writes=[dn], signal=(k == 7))
                    for half, src, sname in ((0, bkB, bkBn), (1, bkO, bkOn)):
                        v = src[:, 0:384].rearrange("p (g ri c) -> p g ri c", g=2, ri=2)
                        for g in range(2):
                            j0_ = half * 96 + g * 48
                            for ri, (Dt, dn) in enumerate(((Dre, 'Dre'), (Dim, 'Dim'))):
                                if half == 0:
                                    T.op('dve', lambda e, v=v, g=g, j0_=j0_, c=c, ri=ri, Dt=Dt: e.tensor_copy(out=Dt[:, j0_:j0_ + 48, 2 * c:2 * c + 2], in_=v[:, g, ri, :].rearrange("p (j l) -> p j l", l=2)), reads=[sname], writes=[dn + '_w%d_%d' % (half, c % 2)])
                                else:
                                    T.op('act', lambda e, v=v, g=g, j0_=j0_, c=c, ri=ri, Dt=Dt: e.activation(out=Dt[:, j0_:j0_ + 48, 2 * c:2 * c + 2], in_=v[:, g, ri, :].rearrange("p (j l) -> p j l", l=2), func=AF.Copy), reads=[sname], writes=[dn + '_w%d_%d' % (half, c % 2)])
                fa, fb = f1['f1a_' + kind], f1['f1b_' + kind]
                tcc, tsp = tw['tcc_' + kind], tw['tsp_' + kind]
                tcc_v = tcc[:].rearrange("p (a b c) -> p a b c", a=2, b=2)
                tsp_v = tsp[:].rearrange("p (a b c) -> p a b c", a=2, b=2)
                for j4 in range(nj4):
                    p2, p2n = (pS[2], 'pS2') if j4 % 2 == 0 else (pS[3], 'pS3')
                    for jj in range(2):
                        pb = pS[jj]
                        pbn = 'pS%d' % jj
                        bi_ = (j4 % 2) * 2 + jj
                        tA, tB = tAs[bi_], tBs[bi_]
                        tan, tbn = 'tA%d' % bi_, 'tB%d' % bi_
                        for pr in range(2):
                            j = j4 * 4 + jj * 2 + pr
                            T.op('pe', lambda e, j=j, pr=pr, pb=pb, fa=fa: e.matmul(pb[:, pr * 256:(pr + 1) * 256], lhsT=Dre[:, j, :], rhs=fa[:], start=True, stop=False),
                                 reads=['Dre_w0_0', 'Dre_w0_1', 'Dre_w1_0', 'Dre_w1_1', 'f1a_' + kind], writes=[pbn], signal=False)
                            T.op('pe', lambda e, j=j, pr=pr, pb=pb, fb=fb: e.matmul(pb[:, pr * 256:(pr + 1) * 256], lhsT=Dim[:, j, :], rhs=fb[:], start=False, stop=True),
                                 reads=['Dim_w0_0', 'Dim_w0_1', 'Dim_w1_0', 'Dim_w1_1', 'f1b_' + kind], writes=[pbn], signal=(pr == 1))
                        psb = Psb[jj]
                        psn = "Psb%d" % jj
                        T.op('act', lambda e, pb=pb, psb=psb: e.activation(out=psb[:], in_=pb[:], func=AF.Copy), reads=[pbn], writes=[psn])
                        pv = psb[:].rearrange("p (a b c) -> p a b c", a=2, b=2)
                        tAv = tA[:].rearrange("p (a b c) -> p a b c", a=2, b=2)
                        tBv = tB[:].rearrange("p (a b c) -> p a b c", a=2, b=2)
                        T.op('dve', lambda e, psb=psb, tcc=tcc, tA=tA: e.tensor_tensor(out=tA[:], in0=psb[:], in1=tcc[:], op=ALU.mult), reads=[psn, 'tcc_' + kind], writes=[tan])
                        T.op('dve', lambda e, pv=pv, tBv=tBv, tsp_v=tsp_v: e.tensor_tensor(out=tBv[:, :, 0, :], in0=pv[:, :, 1, :], in1=tsp_v[:, :, 0, :], op=ALU.mult), reads=[psn, 'tsp_' + kind], writes=[tbn])
                        T.op('dve', lambda e, pv=pv, tBv=tBv, tsp_v=tsp_v: e.tensor_tensor(out=tBv[:, :, 1, :], in0=pv[:, :, 0, :], in1=tsp_v[:, :, 1, :], op=ALU.mult), reads=[psn, 'tsp_' + kind], writes=[tbn])
                        for pr in range(2):
                            q4 = jj * 2 + pr
                            srcs = [(tAv[:, pr, 0, :], c2, tan), (tBv[:, pr, 0, :], c2, tbn), (tAv[:, pr, 1, :], s2, tan), (tBv[:, pr, 1, :], s2, tbn)]
                            for si, (lh, rh, ln) in enumerate(srcs):
                                T.op('pe', lambda e, lh=lh, rh=rh, q4=q4, p2=p2, si=si: e.matmul(p2[:, q4 * 128:(q4 + 1) * 128], lhsT=lh, rhs=rh[:], start=(si == 0), stop=(si == 3)),
                                     reads=[ln, 'c2', 's2'], writes=[p2n], signal=(pr == 1 and si == 3))
                    T.op('act', lambda e, j4=j4, p2=p2: e.activation(out=Ybuf[:, :, 8 * j4:8 * j4 + 8].rearrange("p k (q l) -> p q k l", q=4),
                                                               in_=p2[:].rearrange("p (q k l) -> p q k l", q=4, k=64), func=AF.Copy), reads=[p2n], writes=['Ybuf'])
                for q8 in range(8 if do_yd else 0):
                    if kind == 'p':
                        T.dma('sp', yd[0:8192, :].rearrange("(k2 k1) ch -> k1 k2 ch", k1=128)[:, q8 * 8:(q8 + 1) * 8, :], Ybuf[:, q8 * 8:(q8 + 1) * 8, :], 'yd', reads=['Ybuf'], writes=['yd%s%d' % (kind, q8)])
                    else:
                        for b in range(2):
                            T.dma('sp', yd[8192 + 4096 * b:8192 + 4096 * (b + 1), :].rearrange("(k2 k1) ch -> k1 k2 ch", k1=64)[:, q8 * 8:(q8 + 1) * 8, :], Ybuf[64 * b:64 * b + 64, q8 * 8:(q8 + 1) * 8, :], 'yd', reads=['Ybuf'], writes=['yd%s%d%d' % (kind, q8, b)])
            T.barrier()
            T.emit()
        if stop == 1:
            return nc
        xn_l[1] = xn_l[0]
        xn_names[1] = 'xn0'

        with ExitStack() as es2:
            sb2 = lambda name, shape, dt: es2.enter_context(nc.sbuf_tensor(name, shape, dt))
            Win = sb2("Win", [128, 8, 5504], BF16)
            Wf = sb2("Wf", [128, 3, D], BF16)
            Wn = sb2("Wn", [128, 3, D], BF16)
            Wc = sb2("Wc", [128, 2, D], BF16)
            Wo = sb2("Wo", [128, 8, D], BF16)
            Tint = sb2("Tint", [128, 6, 10, 64], BF16)
            Tful = sb2("Tful", [128, 6, 14, 64], BF16)
            gf_t = sb2("gf_t", [128, D], F32)
            KmT3 = [sb2("KmT%d" % i, [128, 2, 256], BF16) for i in range(3)]
            Vm3 = [sb2("Vm%d" % i, [128, 2, 4, 65], BF16) for i in range(3)]
            with ExitStack() as es2a:
                sb2a = lambda name, shape, dt: es2a.enter_context(nc.sbuf_tensor(name, shape, dt))
                stg = sb2a("stg2", [128, 2752], F32)
                stg_h[0] = stg
                stg_h[1] = sb2a("stg2b", [128, 2752], F32)
                stg_h[2] = sb2a("stg2c", [128, 2752], F32)
                n_stg[0] = 3
                stg_w[0] = 2752
                Wkv_t = sb2a("Wkv", [128, 8, 512], BF16)
                memT = sb2a("memT", [128, 8, 256], BF16)
                load_weight(Wkv_t, w_kv, 8, 512, 'Wkv', scale_t=gmm_t)
                load_weight(Win, w_in, 8, 5504, 'Win', scale_t=gn_t, col0=384)
                load_weight(Wf, w_f, 3, D, 'Wf', scale_c=0.5)
                load_weight(Wn, w_n, 3, D, 'Wn', scale_c=0.5)
                load_weight(Wc, w_c, 2, D, 'Wc', scale_c=0.5)
                load_weight(Wo, w_o, 8, D, 'Wo', scale_c=0.5)
                T.dma('sp', gf_t[:], gfin[:, :], 'gf', writes=['gf'])
                for nm, tab, nj in (('int', Tint, 10), ('full', Tful, 14)):
                    n = 6 * nj * 64
                    tv = tab[:].rearrange("p h j q -> p (h j q)")
                    o = 0
                    while o < n:
                        w = min(1376, n - o)
                        bi = lw_ctr[0] % n_stg[0]
                        lw_ctr[0] += 1
                        sg_ = stg_h[bi]
                        sn = 'stg%d' % bi
                        T.dma('sp', sg_[:, 0:w], eb['eb_' + nm][:, o:o + w], sn, writes=[sn])
                        T.dma('sp', sg_[:, 1376:1376 + w], eb['em_' + nm][:, o:o + w], sn, writes=[sn])
                        T.op('act', lambda e, w=w, sg_=sg_: e.activation(out=sg_[:, 0:w], in_=sg_[:, 0:w], func=AF.Exp), reads=[sn], writes=[sn])
                        T.op('dve', lambda e, w=w, o=o, tv=tv, sg_=sg_: e.tensor_tensor(out=tv[:, o:o + w], in0=sg_[:, 0:w], in1=sg_[:, 1376:1376 + w], op=ALU.mult), reads=[sn], writes=['T' + nm])
                        o += w
                for mi in range(3):
                    KmT, Vm = KmT3[mi], Vm3[mi]
                    T.op('pool', lambda e, Vm=Vm: e.memset(Vm[:, :, :, 64:65], 1.0), writes=['Vm%d' % mi])
                    for mt in range(2):
                        xt = xe[mt % 2]
                        xslot = 'xe%d' % (mt % 2)
                        T.dma('sp', xt[:], mem[mi, mt * 128:(mt + 1) * 128, :], xslot, writes=[xslot])
                        rms_to_hT(xt, xslot, memT[:, :, mt * 128:(mt + 1) * 128], 'memT', par=mt % 2)
                    for hp in range(2):
                        for k in range(8):
                            T.op('pe', lambda e, hp=hp, k=k: e.matmul(pA[:, 0:256], lhsT=Wkv_t[:, k, hp * 128:(hp + 1) * 128], rhs=memT[:, k, :], start=(k == 0), stop=(k == 7)),
                                 reads=['memT', 'Wkv_a', 'Wkv_d'], writes=['pA'], signal=(k == 7))
                        T.op('act', lambda e, hp=hp, KmT=KmT: e.activation(out=KmT[:, hp, :], in_=pA[:, 0:256], func=AF.Copy), reads=['pA'], writes=['KmT%d' % mi])
                    for ck in range(2):
                        for k in range(8):
                            T.op('pe', lambda e, ck=ck, k=k: e.matmul(pB[:, 0:256], lhsT=memT[:, k, ck * 128:(ck + 1) * 128], rhs=Wkv_t[:, k, 256:512], start=(k == 0), stop=(k == 7)),
                                 reads=['memT', 'Wkv_a', 'Wkv_d'], writes=['pB'], signal=(k == 7))
                        T.op('dve', lambda e, ck=ck, Vm=Vm: e.tensor_copy(out=Vm[:, ck, :, 0:64], in_=pB[:, 0:256].rearrange("p (h d) -> p h d", h=4)), reads=['pB'], writes=['Vm%d' % mi])
                T.barrier()
                T.emit()
            if stop == 2:
                return nc
            NH, NR = 4, 6
            hT = [sb2("hT%d" % i, [128, 8, 128], BF16) for i in range(NH)]
            kT = [sb2("kT%d" % i, [128, 3, 128], BF16) for i in range(NR)]
            Vr = [sb2("Vr%d" % i, [128, 6, 65], BF16) for i in range(NR)]
            Pn = [sb2("Pn%d" % i, [128, 6, 128], BF16) for i in range(5)]
            qT = sb2("qT", [128, 3, 128], BF16)
            qcT = sb2("qcT", [128, 2, 128], BF16)
            Pc = sb2("Pc", [128, 8, 128], BF16)
            th = sb2("th", [128, 384], F32)
            rden = sb2("rden", [128, 6], F32)
            Nn = sb2("Nn", [128, 384], F32)
            act_b = sb2("act_b", [128, 384], BF16)
            BT = [{'f': sb2("FT%d" % i, [128, 3, 128], BF16), 'n': sb2("NT%d" % i, [128, 3, 128], BF16), 'c': sb2("CT%d" % i, [128, 2, 128], BF16)} for i in range(2)]
            th2s = [sb2("th2_%d" % i, [128, 512], F32) for i in range(2)]
            ss2 = sb2("ss2", [128, 1], F32)
            ssp2 = sb2("ssp2", [128, 1], F32)
            rstd2 = sb2("rstd2", [128, 1], F32)
            ybuf = sb2("ybuf", [128, 384], BF16)
            acc = sb2("acc", [128, 512], F32)
            mrg = sb2("mrg", [128, D], BF16)
            mT = sb2("mT", [128, 8, 128], BF16)
            xl = sb2("xl", [128, D], F32)
            for i in range(NR):
                T.op('pool', lambda e, i=i: e.memset(Vr[i][:, :, 64:65], 1.0), writes=['Vr%d' % i])

            def proj_tok(dstbank, bname, hslot, hTt, col, n):
                for k in range(8):
                    T.op('pe', lambda e, k=k: e.matmul(dstbank[:, 0:n], lhsT=hTt[:, k, :], rhs=Win[:, k, col:col + n], start=(k == 0), stop=(k == 7)),
                         reads=[hslot, 'Win'], writes=[bname], signal=(k == 7))

            def proj_feat(dstbank, bname, hslot, hTt, col, nch, ntok=128):
                for ch in range(nch):
                    for k in range(8):
                        T.op('pe', lambda e, k=k, ch=ch: e.matmul(dstbank[:, ch * ntok:(ch + 1) * ntok], lhsT=Win[:, k, col + ch * 128:col + (ch + 1) * 128], rhs=hTt[:, k, :], start=(k == 0), stop=(k == 7)),
                             reads=[hslot, 'Win'], writes=[bname], signal=(k == 7 and ch == nch - 1))

            def silu_gate(bank, bname, n):
                T.op('act', lambda e: e.activation(out=th[:, 0:n], in_=bank[:, 0:n], func=AF.Tanh, scale=0.5), reads=[bname], writes=['th'])
                T.op('dve', lambda e: e.scalar_tensor_tensor(out=th[:, 0:n], in0=th[:, 0:n], scalar=1.0, in1=bank[:, 0:n], op0=ALU.add, op1=ALU.mult), reads=['th', bname], writes=['th'])

            def to_featT(nchunk, dst, dname):
                for ch in range(nchunk):
                    T.op('pe', lambda e, ch=ch: e.transpose(out=pT[:, ch, :], in_=act_b[:, ch * 128:(ch + 1) * 128], identity=ident[:]), reads=['act_b', 'ident'], writes=['pT'], signal=(ch == nchunk - 1))
                T.op('act', lambda e: e.activation(out=dst[:], in_=pT[:, 0:nchunk, :], func=AF.Copy), reads=['pT'], writes=[dname])

            seqs = [(xp, yp, 0, 64, 0), (xs[0], ys[0], 8192, 32, 1), (xs[1], ys[1], 12288, 32, 2)]
            def run_seq(xa, ya, ydo, nt, mi):
                KmT, Vm, vmname, kmname = KmT3[mi], Vm3[mi], 'Vm%d' % mi, 'KmT%d' % mi

                def front(t):
                    xt = xe[t % 2]
                    xslot = 'xe%d' % (t % 2)
                    T.dma('sp', xt[:], xa[t * 128:(t + 1) * 128, :], xslot, writes=[xslot])
                    hs = 'hT%d' % (t % NH)
                    rms_to_hT(xt, xslot, hT[t % NH][:], hs, par=t % 2)
                    yield
                    proj_feat(pTf, 'pT', hs, hT[t % NH], OFF['k'], 3)
                    T.op('act', lambda e, t=t: e.activation(out=kT[t % NR][:].rearrange("p c t -> p (c t)"), in_=pTf[:, 0:384], func=AF.Copy), reads=['pT'], writes=['kT%d' % (t % NR)])
                    yield
                    proj_tok(pTf, 'pT', hs, hT[t % NH], OFF['v'], 384)
                    T.op('dve', lambda e, t=t: e.tensor_copy(out=Vr[t % NR][:, :, 0:64], in_=pTf[:, 0:384].rearrange("p (h d) -> p h d", h=6)), reads=['pT'], writes=['Vr%d' % (t % NR)])
                    yield

                def streamX(t):
                    hs = 'hT%d' % (t % NH)
                    h_ = hT[t % NH]
                    bt = BT[t % 2]
                    btn = 'BT%d' % (t % 2)
                    proj_feat(pA, 'pA', hs, h_, OFF['q'], 3)
                    T.op('act', lambda e: e.activation(out=qT[:].rearrange("p c t -> p (c t)"), in_=pA[:, 0:384], func=AF.Copy), reads=['pA'], writes=['qT'])
                    yield
                    if 2 <= t <= nt - 3:
                        keys = [(tau, Tint, 'Tint', 4 - 2 * (tau - t)) for tau in range(t - 2, t + 3)]
                    elif t < 2:
                        keys = [(tau, Tful, 'Tfull', 6 - 2 * (tau - t)) for tau in range(0, 4)]
                    else:
                        keys = [(tau, Tful, 'Tfull', 6 - 2 * (tau - t)) for tau in range(nt - 4, nt)]
                    for i, (tau, tab, tname, j0) in enumerate(keys):
                        kt = kT[tau % NR]
                        kname = 'kT%d' % (tau % NR)
                        b0, b1 = pS[(2 * i) % 3], pS[(2 * i + 1) % 3]
                        n0, n1 = 'pS%d' % ((2 * i) % 3), 'pS%d' % ((2 * i + 1) % 3)
                        for h in (0, 2, 4, 1, 3, 5):
                            bk = b0 if h % 2 == 0 else b1
                            T.op('pe', lambda e, h=h, bk=bk, kt=kt: e.matmul(bk[:, (h // 2) * 128:(h // 2 + 1) * 128], lhsT=kt[64 * (h % 2):64 * (h % 2) + 64, h // 2, :], rhs=qT[64 * (h % 2):64 * (h % 2) + 64, h // 2, :], start=True, stop=True),
                                 reads=[kname, 'qT'], writes=[n0 if h % 2 == 0 else n1], signal=(h >= 4))
                        P = Pn[i]
                        pname = 'Pn%d' % i
                        T.op('act', lambda e, P=P, b0=b0: e.activation(out=P[:, 0:3, :].rearrange("p h q -> p (h q)"), in_=b0[:, 0:384], func=AF.Exp, scale=0.125), reads=[n0], writes=[pname])
                        T.op('act', lambda e, P=P, b1=b1: e.activation(out=P[:, 3:6, :].rearrange("p h q -> p (h q)"), in_=b1[:, 0:384], func=AF.Exp, scale=0.125), reads=[n1, pname], writes=[pname])
                        T.op('dve', lambda e, P=P, tab=tab, j0=j0: e.tensor_tensor(out=P[:].rearrange("p h (a q) -> p h a q", a=2), in0=P[:].rearrange("p h (a q) -> p h a q", a=2), in1=tab[:, :, j0:j0 + 2, :], op=ALU.mult),
                             reads=[pname, tname], writes=[pname])
                        yield
                    nk = len(keys)
                    for h in range(6):
                        for i, (tau, tab, tname, j0) in enumerate(keys):
                            T.op('pe', lambda e, h=h, i=i, tau=tau: e.matmul(pO[:, h * 65:(h + 1) * 65], lhsT=Pn[i][:, (h % 2) * 3 + h // 2, :], rhs=Vr[tau % NR][:, h, :], start=(i == 0), stop=(i == nk - 1)),
                                 reads=['Pn%d' % i, 'Vr%d' % (tau % NR)], writes=['pO'], signal=(i == nk - 1 and h == 5))
                    proj_tok(pA, 'pA', hs, h_, OFF['gna'], 384)
                    silu_gate(pA, 'pA', 384)
                    yield
                    pOv = pO[:, 0:390].rearrange("p (h d) -> p h d", h=6)
                    T.op('dve', lambda e: e.reciprocal(out=rden[:, 0:6], in_=pOv[:, :, 64]), reads=['pO'], writes=['rden'])
                    T.op('dve', lambda e: e.tensor_tensor(out=Nn[:, 0:384].rearrange("p (h d) -> p h d", h=6), in0=pOv[:, :, 0:64], in1=rden[:, 0:6].unsqueeze(2).broadcast_to([128, 6, 64]), op=ALU.mult), reads=['pO', 'rden'], writes=['Nn'])
                    T.op('dve', lambda e: e.tensor_tensor(out=act_b[:, 0:384], in0=Nn[:, 0:384], in1=th[:, 0:384], op=ALU.mult), reads=['Nn', 'th'], writes=['act_b'])
                    to_featT(3, bt['n'], btn + 'n')
                    yield
                    proj_feat(pA, 'pA', hs, h_, OFF['qca'], 2)
                    T.op('act', lambda e: e.activation(out=qcT[:].rearrange("p c t -> p (c t)"), in_=pA[:, 0:256], func=AF.Copy), reads=['pA'], writes=['qcT'])
                    yield
                    for par in range(2):
                        for h in (par, par + 2):
                            for ck in range(2):
                                sl = (h // 2) * 2 + ck
                                T.op('pe', lambda e, ck=ck, h=h, sl=sl, par=par: e.matmul(pS[par][:, sl * 128:(sl + 1) * 128], lhsT=KmT[64 * (h % 2):64 * (h % 2) + 64, h // 2, ck * 128:(ck + 1) * 128], rhs=qcT[64 * (h % 2):64 * (h % 2) + 64, h // 2, :], start=True, stop=True),
                                     reads=[kmname, 'qcT'], writes=['pS%d' % par], signal=(sl == 3))
                        T.op('act', lambda e, par=par: e.activation(out=Pc[:, par * 4:(par + 1) * 4, :].rearrange("p h q -> p (h q)"), in_=pS[par][:, :], func=AF.Exp, scale=0.125), reads=['pS%d' % par, 'Pc'], writes=['Pc'])
                    yield
                    for h in range(4):
                        for ck in range(2):
                            T.op('pe', lambda e, h=h, ck=ck: e.matmul(pS[2][:, h * 65:(h + 1) * 65], lhsT=Pc[:, (h % 2) * 4 + (h // 2) * 2 + ck, :], rhs=Vm[:, ck, h, :], start=(ck == 0), stop=(ck == 1)),
                                 reads=['Pc', vmname], writes=['pS2'], signal=(ck == 1 and h == 3))
                    proj_tok(pS[0], 'pS0', hs, h_, OFF['gca'], 256)
                    silu_gate(pS[0], 'pS0', 256)
                    yield
                    pBv = pS[2][:, 0:260].rearrange("p (h d) -> p h d", h=4)
                    T.op('dve', lambda e: e.reciprocal(out=rden[:, 0:4], in_=pBv[:, :, 64]), reads=['pS2'], writes=['rden'])
                    T.op('dve', lambda e: e.tensor_tensor(out=Nn[:, 0:256].rearrange("p (h d) -> p h d", h=4), in0=pBv[:, :, 0:64], in1=rden[:, 0:4].unsqueeze(2).broadcast_to([128, 4, 64]), op=ALU.mult), reads=['pS2', 'rden'], writes=['Nn'])
                    T.op('dve', lambda e: e.tensor_tensor(out=act_b[:, 0:256], in0=Nn[:, 0:256], in1=th[:, 0:256], op=ALU.mult), reads=['Nn', 'th'], writes=['act_b'])
                    to_featT(2, bt['c'], btn + 'c')
                    yield
                    T.dma('sp', ybuf[:], yd[ydo + t * 128:ydo + (t + 1) * 128, :], 'ybuf', writes=['ybuf'])
                    proj_tok(pO, 'pO', hs, h_, OFF['gf'], 384)
                    silu_gate(pO, 'pO', 384)
                    T.op('dve', lambda e: e.tensor_tensor(out=act_b[:, 0:384], in0=ybuf[:, :], in1=th[:, 0:384], op=ALU.mult), reads=['ybuf', 'th'], writes=['act_b'])
                    to_featT(3, bt['f'], btn + 'f')
                    yield

                def streamY(t):
                    hs = 'hT%d' % (t % NH)
                    h_ = hT[t % NH]
                    bt = BT[t % 2]
                    btn = 'BT%d' % (t % 2)
                    T.dma('sp', xl[:], xa[t * 128:(t + 1) * 128, :], 'xl', writes=['xl'])
                    for half in range(2):
                        for bi, (bk, W_, nkc) in enumerate((('f', Wf, 3), ('n', Wn, 3), ('c', Wc, 2))):
                            col = GM_OFF + bi * 1024 + half * 512
                            th2 = th2s[(half * 3 + bi) % 2]
                            thn = 'th2_%d' % ((half * 3 + bi) % 2)
                            proj_tok(pB, 'pB', hs, h_, col, 512)
                            T.op('act', lambda e, th2=th2: e.activation(out=th2[:, :], in_=pB[:, :], func=AF.Tanh, scale=0.5), reads=['pB'], writes=[thn])
                            for kc in range(nkc):
                                T.op('pe', lambda e, kc=kc, bk=bk, W_=W_, half=half, bt=bt, nkc=nkc: e.matmul(pS[3][:, :], lhsT=bt[bk][:, kc, :], rhs=W_[:, kc, half * 512:(half + 1) * 512], start=(kc == 0), stop=(kc == nkc - 1)),
                                     reads=[btn + bk, 'W' + bk], writes=['pS3'], signal=(kc == nkc - 1))
                            yield
                            if bi == 0:
                                T.op('dve', lambda e, th2=th2: e.scalar_tensor_tensor(out=acc[:, :], in0=th2[:, :], scalar=1.0, in1=pS[3][:, :], op0=ALU.add, op1=ALU.mult), reads=[thn, 'pS3'], writes=['acc'])
                            else:
                                T.op('dve', lambda e, th2=th2: e.scalar_tensor_tensor(out=th2[:, :], in0=th2[:, :], scalar=1.0, in1=pS[3][:, :], op0=ALU.add, op1=ALU.mult), reads=[thn, 'pS3'], writes=[thn])
                                if bi == 1:
                                    T.op('dve', lambda e, th2=th2: e.tensor_tensor(out=acc[:, :], in0=acc[:, :], in1=th2[:, :], op=ALU.add), reads=['acc', thn], writes=['acc'])
                                else:
                                    T.op('dve', lambda e, half=half, th2=th2: e.tensor_tensor(out=mrg[:, half * 512:(half + 1) * 512], in0=acc[:, :], in1=th2[:, :], op=ALU.add), reads=['acc', thn], writes=['mrg'])
                    yield
                    for k in range(8):
                        T.op('pe', lambda e, k=k: e.transpose(out=pT[:, k, :], in_=mrg[:, k * 128:(k + 1) * 128], identity=ident[:]), reads=['mrg', 'ident'], writes=['pT'], signal=(k == 7))
                    T.op('act', lambda e: e.activation(out=mT[:].rearrange("p k t -> p (k t)"), in_=pT[:].rearrange("p k t -> p (k t)"), func=AF.Copy), reads=['pT'], writes=['mT'])
                    yield
                    for half in range(2):
                        ob, obn = (pB, 'pB') if half == 0 else (pS[3], 'pS3')
                        for k in range(8):
                            T.op('pe', lambda e, k=k, half=half, ob=ob: e.matmul(ob[:, :], lhsT=mT[:, k, :], rhs=Wo[:, k, half * 512:(half + 1) * 512], start=(k == 0), stop=(k == 7)),
                                 reads=['mT', 'Wo'], writes=[obn], signal=(k == 7))
                    for half in range(2):
                        ob, obn = (pB, 'pB') if half == 0 else (pS[3], 'pS3')
                        T.op('dve', lambda e, half=half, ob=ob: e.tensor_tensor(out=xl[:, half * 512:(half + 1) * 512], in0=xl[:, half * 512:(half + 1) * 512], in1=ob[:, :], op=ALU.add), reads=['xl', obn], writes=['xl'])
                    if True:
                        yield
                    T.op('act', lambda e: e.activation(out=mrg[:], in_=xl[:], func=AF.Square, scale=1.0 / 32, accum_out=ss2[:]), reads=['xl'], writes=['mrg', 'ss2'])
                    T.op('dve', lambda e: e.tensor_scalar(out=ssp2[:], in0=ss2[:], scalar1=EPS, scalar2=None, op0=ALU.add), reads=['ss2'], writes=['ssp2'])
                    T.op('pool', lambda e: e.tensor_tensor(out=rstd2[:], in0=ssp2[:], in1=mh[:], op=ALU.pow), reads=['ssp2', 'mh'], writes=['rstd2'])
                    T.op('dve', lambda e: e.scalar_tensor_tensor(out=xl[:], in0=xl[:], scalar=rstd2[:, 0:1], in1=gf_t[:], op0=ALU.mult, op1=ALU.mult), reads=['xl', 'rstd2', 'gf'], writes=['xl'])
                    T.dma('pool', ya[t * 128:(t + 1) * 128, :], xl[:], 'out', reads=['xl'])
                    yield

                def interleave(specs):
                    live = [[g, d] for g, d in specs if g is not None]
                    rnd = 0
                    while live:
                        for it in list(live):
                            if it[1] > rnd:
                                continue
                            try:
                                next(it[0])
                            except StopIteration:
                                live.remove(it)
                        rnd += 1

                def run(g):
                    for _ in g:
                        pass

                for tt in range(3):
                    run(front(tt))
                for t in range(nt):
                    if t + 3 < nt:
                        run(front(t + 3))
                    run(streamX(t))
                    run(streamY(t))

            for sq_ in seqs:
                if ntl is not None:
                    sq_ = (sq_[0], sq_[1], sq_[2], ntl, sq_[4])
                run_seq(*sq_)
            T.barrier()
            T.emit()
    nc._trk_stats = T.stats
    return nc


_NC_CACHE = {}


def kernel(x_prompt, x_sample, mem_prompt, mem_sample, g_norm, w_in, na_rpb, g_mem, w_mem_kv,
           w_f_out, w_na_out, w_ca_out, w_out, g_final):
    f = lambda a: np.ascontiguousarray(np.asarray(a, dtype=np.float32))
    x_prompt, x_sample, mem_prompt, mem_sample = f(x_prompt), f(x_sample), f(mem_prompt), f(mem_sample)
    if 'nc' not in _NC_CACHE:
        _NC_CACHE['nc'] = build_nc()
    nc = _NC_CACHE['nc']
    shared = {
        "gn": f(np.asarray(g_norm)[0].reshape(8, 128).T),
        "gmm": f(np.asarray(g_mem)[0].reshape(8, 128).T),
        "gfin": f(np.broadcast_to(np.asarray(g_final)[None, :], (128, D))),
        "w_in": f(np.asarray(w_in)[0]), "w_kv": f(np.asarray(w_mem_kv)[0]), "w_f": f(np.asarray(w_f_out)[0]),
        "w_n": f(np.asarray(w_na_out)[0]), "w_c": f(np.asarray(w_ca_out)[0]), "w_o": f(np.asarray(w_out)[0]),
    }
    for k, v in _consts().items():
        shared["c_" + k] = f(v)
    shared.update(_build_etabs(np.asarray(na_rpb, dtype=np.float32)[0]))
    in_maps = []
    for i in range(NCORES):
        m = dict(shared)
        m["xp"] = x_prompt[i]
        m["xs"] = x_sample[2 * i:2 * i + 2]
        m["mem"] = f(np.concatenate([mem_prompt[i:i + 1], mem_sample[2 * i:2 * i + 2]], 0))
        in_maps.append(m)
    res = run_bass_kernel_spmd(nc, in_maps, core_ids=list(range(NCORES)))
    y_p = np.stack([np.asarray(r["yp"], dtype=np.float32) for r in res.results], 0)
    y_s = np.concatenate([np.asarray(r["ys"], dtype=np.float32) for r in res.results], 0)
    return (y_p, y_s)
```

```python
import numpy as np
from contextlib import ExitStack
import concourse.bass as bass
import concourse.mybir as mybir
from concourse.bass_utils import run_bass_kernel_spmd

F32 = mybir.dt.float32
BF16 = mybir.dt.bfloat16
AF = mybir.ActivationFunctionType
ALU = mybir.AluOpType

D = 1024
EPS = 1e-6
NCORES = 8
GM_OFF = 2432
OFF = dict(gf=0, q=384, k=768, v=1152, gna=1536, qca=1920, gca=2176)


USE_RANK = True
PSUM_SLOTS = {'pT', 'pA', 'pB', 'pS0', 'pS1', 'pS2', 'pS3', 'pO'}


class Slot:
    def __init__(self, name):
        self.name = name
        self.w = []
        self.r = []


class _Probe:
    def __init__(self):
        self.rec = None

    def __getattr__(self, name):
        def f(*a, **k):
            self.rec = (name, a, k)
            return self
        return f


def _free_size(ap):
    n = 1
    for d in ap.shape[1:]:
        n *= int(d)
    return n


def _auto_cost(eng, fn):
    p = _Probe()
    try:
        fn(p)
        name, a, k = p.rec
        out = k.get('out', a[0] if a else None)
        n = _free_size(out)
        if eng == 'pe':
            if name == 'transpose':
                return 0.085
            return max(0.057, n / 2400.0 + 0.012)
        if eng == 'act':
            return 0.17 + n / 1200.0 + (0.1 if 'accum_out' in k else 0.0)
        if eng == 'dve':
            return 0.1 + n / 1400.0
        if eng == 'pool':
            return 0.25 + n / 450.0
    except Exception:
        pass
    return None


class Trk:
    ENG = ['pe', 'act', 'dve', 'pool', 'sp']
    COST = {'pe': 0.12, 'act': 0.6, 'dve': 0.5, 'pool': 0.9, 'sp': 2.5}
    HOP = 0.2
    HOP_SAME = 0.08
    WINDOW = 120

    def __init__(self, nc, es):
        self.nc = nc
        self.es = es
        self.psem = {e: es.enter_context(nc.semaphore('prog_' + e)) for e in ['pe', 'act', 'dve', 'pool']}
        self.cnt = {e: 0 for e in self.psem}
        self.dsem = {}
        self.dcnt = {}
        self.slots = {}
        self.units = []
        self.pend = {}
        self.waited = {e: {} for e in self.ENG}
        self.reorder = True
        self.use_rank = USE_RANK
        self.diag = None

    def S(self, name):
        if name not in self.slots:
            self.slots[name] = Slot(name)
        return self.slots[name]

    def _finish_unit(self, eng, fns, reads, writes, cost, semname=None):
        uid = len(self.units)
        deps = set()
        for s in reads:
            deps.update(s.w)
            if s.name in PSUM_SLOTS:
                deps.update(s.r)
        for s in writes:
            deps.update(s.w)
            deps.update(s.r)
        deps.discard(uid)
        self.units.append(dict(eng=eng, fns=fns, deps=deps, cost=cost, sem=semname, rn=[x.name for x in reads], wn=[x.name for x in writes]))
        for s in reads:
            s.r.append(uid)
        for s in writes:
            s.w = [uid]
            s.r = []
        return uid

    def op(self, eng, fn, reads=(), writes=(), signal=True, cost=None):
        reads = [self.S(n) for n in reads]
        writes = [self.S(n) for n in writes]
        p = self.pend.setdefault(eng, dict(fns=[], reads=[], writes=[], cost=0.0))
        p['fns'].append(fn)
        p['reads'] += [x for x in reads if x not in p['reads']]
        p['writes'] += [x for x in writes if x not in p['writes']]
        if cost is None:
            cost = _auto_cost(eng, fn)
        p['cost'] += (self.COST[eng] if cost is None else cost)
        if signal:
            del self.pend[eng]
            self._finish_unit(eng, p['fns'], p['reads'], p['writes'], p['cost'])

    def dma(self, eng, out, in_, semname, reads=(), writes=(), cost=None):
        reads = [self.S(n) for n in reads]
        writes = [self.S(n) for n in writes]
        if semname not in self.dsem:
            self.dsem[semname] = self.es.enter_context(self.nc.semaphore('d_' + semname))
            self.dcnt[semname] = 0
        fn = lambda e, out=out, in_=in_: e.dma_start(out=out, in_=in_)
        if cost is None:
            try:
                cost = 2.0 + out.nbytes() / 250e3
            except Exception:
                cost = self.COST['sp']
        self._finish_unit(eng, [fn], reads, writes, cost, semname=semname)

    def barrier(self):
        pass

    def _schedule(self):
        U = self.units
        n = len(U)
        order = {e: [] for e in self.ENG}
        if not self.reorder:
            for i, u in enumerate(U):
                order[u['eng']].append(i)
            return order
        ndep = [len(u['deps']) for u in U]
        users = [[] for _ in range(n)]
        for i, u in enumerate(U):
            for d in u['deps']:
                users[d].append(i)
        ready = [0.0 if ndep[i] == 0 else None for i in range(n)]
        fin = [None] * n
        queue = {e: [] for e in self.ENG}
        for i, u in enumerate(U):
            queue[u['eng']].append(i)
        head = {e: 0 for e in self.ENG}
        done = [False] * n
        free_at = {e: 0.0 for e in self.ENG}
        left = n
        W = self.WINDOW
        rank = [0.0] * n
        if self.use_rank:
            for i in range(n - 1, -1, -1):
                r = 0.0
                for k in users[i]:
                    v = rank[k] + (self.HOP_SAME if U[k]['eng'] == U[i]['eng'] else self.HOP)
                    if v > r:
                        r = v
                rank[i] = r + U[i]['cost']
        while left:
            best = None
            for e in self.ENG:
                q = queue[e]
                h = head[e]
                while h < len(q) and done[q[h]]:
                    h += 1
                head[e] = h
                fa = free_at[e]
                cnt = 0
                j = h
                cand = None
                while j < len(q) and cnt < W:
                    i = q[j]
                    j += 1
                    if done[i]:
                        continue
                    cnt += 1
                    r = ready[i]
                    if r is None:
                        continue
                    st = r if r > fa else fa
                    key = (st, -rank[i], i)
                    if cand is None or key < cand:
                        cand = key
                    if not self.use_rank and r <= fa:
                        break
                if cand is not None:
                    key = (cand[0], cand[1], cand[2], e)
                    if best is None or key[:3] < best[:3]:
                        best = key
            best = (best[0], best[2], best[3])
            st, i, e = best
            if self.diag is not None and st > free_at[e] + 0.2 and U[i]['deps']:
                d = max(U[i]['deps'], key=lambda d: fin[d])
                key = (e, U[d]['eng'], tuple(U[d]['wn'][:2]), tuple(U[i]['wn'][:1]))
                self.diag[key] = self.diag.get(key, 0.0) + (st - free_at[e])
            f = st + U[i]['cost']
            fin[i] = f
            free_at[e] = (st + 0.15) if U[i]['sem'] is not None else f
            done[i] = True
            order[e].append(i)
            left -= 1
            for k in users[i]:
                ndep[k] -= 1
                if ndep[k] == 0:
                    ek = U[k]['eng']
                    ready[k] = max(fin[d] + (self.HOP_SAME if U[d]['eng'] == ek else self.HOP) for d in U[k]['deps'])
        self.sim_time = max(free_at.values())
        return order

    def emit(self):
        nc = self.nc
        assert not self.pend, self.pend.keys()
        U = self.units
        order = self._schedule()
        self.stats = getattr(self, 'stats', []) + [(len(U), {e: len(order[e]) for e in self.ENG}, getattr(self, 'sim_time', None))]
        tok = [None] * len(U)
        for e in self.ENG:
            for i in order[e]:
                u = U[i]
                if u['sem'] is not None:
                    self.dcnt[u['sem']] += 16
                    tok[i] = (u['sem'], self.dcnt[u['sem']])
                else:
                    self.cnt[e] += 1
                    tok[i] = (e, self.cnt[e])
        lists = {e: [] for e in self.ENG}
        for e in self.ENG:
            L = lists[e]
            wd = self.waited[e]
            for i in order[e]:
                u = U[i]
                need = {}
                for d in u['deps']:
                    k, v = tok[d]
                    if k == 'pe' and e == 'pe':
                        continue
                    if v > need.get(k, 0):
                        need[k] = v
                for k, v in need.items():
                    if wd.get(k, 0) >= v:
                        continue
                    wd[k] = v
                    sem = self.psem[k] if k in self.psem else self.dsem[k]
                    L.append(lambda en, sem=sem, v=v: en.wait_ge(sem, v))
                fns = u['fns']
                for fn in fns[:-1]:
                    L.append(fn)
                if u['sem'] is not None:
                    sem = self.dsem[u['sem']]
                    L.append(lambda en, fn=fns[-1], sem=sem: fn(en).then_inc(sem, 16))
                else:
                    sem = self.psem[e]
                    L.append(lambda en, fn=fns[-1], sem=sem: fn(en).then_inc(sem, 1))
            for k, v in list(self.cnt.items()) + list(self.dcnt.items()):
                if v > 0 and wd.get(k, 0) < v:
                    wd[k] = v
                    sem = self.psem[k] if k in self.psem else self.dsem[k]
                    L.append(lambda en, sem=sem, v=v: en.wait_ge(sem, v))
        with nc.Block() as block:
            @block.tensor
            def _(en):
                for f in lists['pe']:
                    f(en)

            @block.scalar
            def _(en):
                for f in lists['act']:
                    f(en)

            @block.vector
            def _(en):
                for f in lists['dve']:
                    f(en)

            @block.gpsimd
            def _(en):
                for f in lists['pool']:
                    f(en)

            @block.sync
            def _(en):
                for f in lists['sp']:
                    f(en)
        self.units = []
        for s in self.slots.values():
            s.w = []
            s.r = []


def _consts():
    c = {}
    a = np.arange(96)
    ang = 2 * np.pi * np.outer(a, a) / 96
    c['cs96'] = np.concatenate([np.cos(ang), -np.sin(ang)], 1).astype(np.float32)
    r = np.arange(128)
    ang = 2 * np.pi * np.outer(r, r) / 128
    sc = 1.0 / np.sqrt(96.0 * 8192.0)
    C, S_ = np.cos(ang) * sc, np.sin(ang) * sc
    c['f1a_p'] = np.concatenate([C, -S_], 1).astype(np.float32)
    c['f1b_p'] = np.concatenate([S_, C], 1).astype(np.float32)
    r64 = np.arange(64)
    ang = 2 * np.pi * np.outer(r64, r64) / 64
    sc = 1.0 / np.sqrt(96.0 * 4096.0)
    C64, S64 = np.cos(ang) * sc, np.sin(ang) * sc
    Cb = np.zeros((128, 128)); Sb = np.zeros((128, 128))
    for b in range(2):
        Cb[64 * b:64 * b + 64, 64 * b:64 * b + 64] = C64
        Sb[64 * b:64 * b + 64, 64 * b:64 * b + 64] = S64
    c['f1a_s'] = np.concatenate([Cb, -Sb], 1).astype(np.float32)
    c['f1b_s'] = np.concatenate([Sb, Cb], 1).astype(np.float32)
    m = np.arange(128)
    cc = m // 2
    th_p = 2 * np.pi * np.outer(cc, np.arange(128)) / 8192.0
    th_s = 2 * np.pi * np.outer(cc, np.arange(128) % 64) / 4096.0
    for nm, th in (('p', th_p), ('s', th_s)):
        tc = np.cos(th); ts = np.sin(th)
        tcc = np.stack([np.stack([tc, tc], 1)] * 2, 1)
        tsp = np.stack([np.stack([ts, -ts], 1)] * 2, 1)
        c['tcc_' + nm] = tcc.astype(np.float32)
        c['tsp_' + nm] = tsp.astype(np.float32)
    ang = 2 * np.pi * np.outer(r64, r64) / 64
    C2 = np.zeros((64, 2, 64, 2)); S2 = np.zeros((64, 2, 64, 2))
    for l in range(2):
        C2[:, l, :, l] = np.cos(ang)
        S2[:, l, :, l] = np.sin(ang)
    c['c2'] = C2.reshape(128, 128).astype(np.float32)
    c['s2'] = S2.reshape(128, 128).astype(np.float32)
    c['ident'] = np.eye(128, dtype=np.float32)
    return c


def _etab_index():
    kc = np.arange(64)[:, None]
    qc = np.arange(64)[None, :]
    cs = np.clip(qc - 8, 0, 48)
    colvalid = ((kc - cs) >= 0) & ((kc - cs) < 16)
    dc = np.clip(kc - qc + 15, 0, 30)
    out = {}
    for nm, nj, c0, lo, hi in (('int', 10, 4, -4, 3), ('full', 14, 6, -7, 7)):
        dri = np.zeros((2, nj), np.int64)
        rv = np.zeros((2, nj), bool)
        for kh in range(2):
            for j in range(nj):
                dr = kh - j + c0
                rv[kh, j] = (lo <= dr <= hi)
                dri[kh, j] = np.clip(dr, -7, 7) + 7
        mask = (rv[:, None, :, None] & colvalid[None, :, None, :])
        out[nm] = (dri, mask.astype(np.float32), nj)
    return out, dc


def _build_etabs(rpb):
    idx, dc = _etab_index()
    res = {}
    for nm in ('int', 'full'):
        dri, mask, nj = idx[nm]
        g = rpb[:, dri[:, :, None, None], dc[None, None, :, :]]
        g = np.transpose(g, (1, 3, 0, 2, 4))
        g = g[:, :, [0, 2, 4, 1, 3, 5]]
        res['eb_' + nm] = np.ascontiguousarray(g.reshape(128, 6 * nj * 64)).astype(np.float32)
        m = np.broadcast_to(mask[:, :, None, :, :], (2, 64, 6, nj, 64))
        res['em_' + nm] = np.ascontiguousarray(m.reshape(128, 6 * nj * 64)).astype(np.float32)
    return res


def build_nc(stop=0, ntl=None, ncol=64, nj4=48, do_yd=True, kinds=('p', 's'), lvl=9, lvl2=9):
    nc = bass.Bass("TRN2", target_bir_lowering=False)
    dt_in = lambda name, shape: nc.dram_tensor(name, list(shape), F32, kind="ExternalInput").ap()
    xp = dt_in("xp", [8192, D])
    xs = dt_in("xs", [2, 4096, D])
    mem = dt_in("mem", [3, 256, D])
    gn = dt_in("gn", [128, 8])
    gmm = dt_in("gmm", [128, 8])
    gfin = dt_in("gfin", [128, D])
    w_in = dt_in("w_in", [D, 5888])
    w_kv = dt_in("w_kv", [D, 512])
    w_f = dt_in("w_f", [384, D])
    w_n = dt_in("w_n", [384, D])
    w_c = dt_in("w_c", [256, D])
    w_o = dt_in("w_o", [D, D])
    cst = {k: dt_in("c_" + k, v.shape) for k, v in _consts().items()}
    eb = {}
    for nm, nj in (('int', 10), ('full', 14)):
        eb['eb_' + nm] = dt_in("eb_" + nm, [128, 6 * nj * 64])
        eb['em_' + nm] = dt_in("em_" + nm, [128, 6 * nj * 64])
    yp = nc.dram_tensor("yp", [8192, D], F32, kind="ExternalOutput").ap()
    ys = nc.dram_tensor("ys", [2, 4096, D], F32, kind="ExternalOutput").ap()
    yd = nc.dram_tensor("yscr", [16384, 384], BF16).ap()

    with ExitStack() as es:
        T = Trk(nc, es)
        sb = lambda name, shape, dt: es.enter_context(nc.sbuf_tensor(name, shape, dt))
        ps = lambda name, shape, dt: es.enter_context(nc.psum_tensor(name, shape, dt))
        pTf = ps("pT", [128, 512], F32)
        pT = pTf[:].bitcast(BF16).rearrange("p (k t) -> p k t", k=8)
        pA = ps("pA", [128, 512], F32)
        pB = ps("pB", [128, 512], F32)
        pS = [ps("pS%d" % i, [128, 512], F32) for i in range(4)]
        pO = ps("pO", [128, 512], F32)
        ident = sb("ident", [128, 128], BF16)
        mh = sb("mh", [128, 1], F32)
        gn_t = sb("gn_t", [128, 8], F32)
        gmm_t = sb("gmm_t", [128, 8], F32)
        ss_l = [sb("ss%d" % i, [128, 1], F32) for i in range(2)]
        ssp_l = [sb("ssp%d" % i, [128, 1], F32) for i in range(2)]
        rstd_l = [sb("rstd%d" % i, [128, 1], F32) for i in range(2)]
        xn_l = [sb("xn0", [128, D], BF16), None]
        xn_names = ['xn0', 'xn1']
        xe = [sb("xe%d" % i, [128, D], F32) for i in range(2)]

        stg_h = [None, None, None, None]
        n_stg = [2]
        stg_w = [1536]

        def load_const(dst, src_ap, shape2, name):
            stg = stg_h[0]
            n = shape2
            o = 0
            while o < n:
                w = min(1536, n - o)
                T.dma('sp', stg[:, 0:w], src_ap[:, o:o + w], 'stg0', writes=['stg0'])
                T.op('dve', lambda e, o=o, w=w: e.tensor_copy(out=dst[:, o:o + w], in_=stg[:, 0:w]), reads=['stg0'], writes=[name])
                o += w

        T.op('dve', lambda e: e.memset(mh[:], -0.5), writes=['mh'])
        T.dma('sp', gn_t[:], gn[:, :], 'gn', writes=['gn'])
        T.dma('sp', gmm_t[:], gmm[:, :], 'gmm', writes=['gmm'])

        pT2 = pS[0][:].bitcast(BF16).rearrange("p (k t) -> p k t", k=8)

        def rms_to_hT(xt, xslot, hT_dst, hslot, par=0, ptv=None, ptn='pT', copy_eng='dve'):
            if ptv is None:
                ptv = pT
            xs_ = xslot if isinstance(xslot, list) else [xslot]
            xn, ss, ssp, rstd = xn_l[par], ss_l[par], ssp_l[par], rstd_l[par]
            xnn, ssn, sspn, rsn = xn_names[par], 'ss%d' % par, 'ssp%d' % par, 'rstd%d' % par
            T.op('act', lambda e: e.activation(out=xn[:], in_=xt[:], func=AF.Square, scale=1.0 / 32, accum_out=ss[:]), reads=xs_, writes=[xnn, ssn])
            T.op('dve', lambda e: e.tensor_scalar(out=ssp[:], in0=ss[:], scalar1=EPS, scalar2=None, op0=ALU.add), reads=[ssn], writes=[sspn])
            T.op('pool', lambda e: e.tensor_tensor(out=rstd[:], in0=ssp[:], in1=mh[:], op=ALU.pow), reads=[sspn, 'mh'], writes=[rsn])
            T.op('dve', lambda e: e.tensor_scalar(out=xn[:], in0=xt[:], scalar1=rstd[:, 0:1], scalar2=None, op0=ALU.mult), reads=xs_ + [rsn], writes=[xnn])
            for k in range(8):
                T.op('pe', lambda e, k=k: e.transpose(out=ptv[:, k, :], in_=xn[:, k * 128:(k + 1) * 128], identity=ident[:]), reads=[xnn, 'ident'], writes=[ptn], signal=(k == 7))
            if copy_eng == 'dve':
                T.op('dve', lambda e: e.tensor_copy(out=hT_dst, in_=ptv[:]), reads=[ptn], writes=[hslot])
            else:
                T.op('act', lambda e: e.activation(out=hT_dst, in_=ptv[:], func=AF.Copy), reads=[ptn], writes=[hslot])

        lw_ctr = [0]

        def load_weight(dst, src, nk, ncols, name, scale_t=None, scale_c=None, col0=0):
            cw = stg_w[0]
            for k in range(nk):
                o = 0
                while o < ncols:
                    w = min(cw, ncols - o)
                    bi = lw_ctr[0] % n_stg[0]
                    lw_ctr[0] += 1
                    stg = stg_h[bi]
                    sn = 'stg%d' % bi
                    T.dma('sp', stg[:, 0:w], src[k * 128:(k + 1) * 128, col0 + o:col0 + o + w], sn, writes=[sn])
                    po = 0
                    pi = 0
                    while po < w:
                        pw = min(1408, w - po)
                        eng = 'act' if (pi + bi) % 2 == 0 else 'dve'
                        pi += 1
                        oo = o + po
                        if eng == 'act':
                            if scale_t is not None:
                                T.op('act', lambda e, k=k, oo=oo, po=po, pw=pw, stg=stg: e.activation(out=dst[:, k, oo:oo + pw], in_=stg[:, po:po + pw], func=AF.Copy, scale=scale_t[:, k:k + 1]), reads=[sn, 'gn', 'gmm'], writes=[name + '_a'])
                            else:
                                sc = 1.0 if scale_c is None else float(scale_c)
                                T.op('act', lambda e, k=k, oo=oo, po=po, pw=pw, stg=stg, sc=sc: e.activation(out=dst[:, k, oo:oo + pw], in_=stg[:, po:po + pw], func=AF.Copy, scale=sc), reads=[sn], writes=[name + '_a'])
                        else:
                            if scale_t is not None:
                                T.op('dve', lambda e, k=k, oo=oo, po=po, pw=pw, stg=stg: e.tensor_scalar(out=dst[:, k, oo:oo + pw], in0=stg[:, po:po + pw], scalar1=scale_t[:, k:k + 1], scalar2=None, op0=ALU.mult), reads=[sn, 'gn', 'gmm'], writes=[name + '_d'])
                            else:
                                sc = 1.0 if scale_c is None else float(scale_c)
                                T.op('dve', lambda e, k=k, oo=oo, po=po, pw=pw, stg=stg, sc=sc: e.tensor_scalar(out=dst[:, k, oo:oo + pw], in0=stg[:, po:po + pw], scalar1=sc, scalar2=None, op0=ALU.mult), reads=[sn], writes=[name + '_d'])
                        po += pw
                    o += w

        with ExitStack() as es1:
            sb1 = lambda name, shape, dt: es1.enter_context(nc.sbuf_tensor(name, shape, dt))
            xn_l[1] = sb1("xn1", [128, D], BF16)
            Dre = sb1("Dre", [128, 192, 128], BF16)
            Ybuf = sb1("Ybuf", [128, 64, 384], BF16)
            Dim = sb1("Dim", [128, 192, 128], BF16)
            f1 = {k: sb1(k, [128, 256], BF16) for k in ('f1a_p', 'f1b_p', 'f1a_s', 'f1b_s')}
            tw = {k: sb1(k, [128, 512], BF16) for k in ('tcc_p', 'tsp_p', 'tcc_s', 'tsp_s')}
            c2 = sb1("c2", [128, 128], BF16)
            s2 = sb1("s2", [128, 128], BF16)
            hT1_l = [sb1("h1T_p1_%d" % i, [128, 8, 128], BF16) for i in range(2)]
            tAs = [sb1("tA%d" % i, [128, 512], BF16) for i in range(4)]
            tBs = [sb1("tB%d" % i, [128, 512], BF16) for i in range(4)]
            Psb = [sb1("Psb%d" % i, [128, 512], BF16) for i in range(2)]
            Wp = sb1("Wp", [128, 8, 768], BF16)
            with ExitStack() as es1a:
                sb1a = lambda name, shape, dt: es1a.enter_context(nc.sbuf_tensor(name, shape, dt))
                stg = sb1a("stg", [128, 1536], F32)
                stg_h[0] = stg
                n_stg[0] = 1
                Wf1 = sb1a("Wf1", [128, 8, 384], BF16)
                cs96 = sb1a("cs96", [96, 192], BF16)
                WfT = sb1a("WfT", [96, 4, 128], BF16)
                load_const(ident[:], cst['ident'], 128, 'ident')
                load_weight(Wf1, w_in, 8, 384, 'Wf1', scale_t=gn_t)
                T.dma('sp', stg[0:96, 0:192], cst['cs96'][:, :], 'stg0', writes=['stg0'])
                T.op('dve', lambda e: e.tensor_copy(out=cs96[:], in_=stg[0:96, 0:192]), reads=['stg0'], writes=['cs96'])
                for k in range(8):
                    for g in range(4):
                        T.op('pe', lambda e, k=k, g=g: e.transpose(out=pT[0:96, g, :], in_=Wf1[:, k, g * 96:(g + 1) * 96], identity=ident[:]), reads=['Wf1_a', 'Wf1_d', 'ident'], writes=['pT'], signal=(g == 3))
                    T.op('act', lambda e: e.activation(out=WfT[:], in_=pT[0:96, 0:4, :], func=AF.Copy), reads=['pT'], writes=['WfT'])
                    for g in range(4):
                        dst, dn = (pA, 'pA') if g < 2 else (pB, 'pB')
                        o = (g % 2) * 192
                        T.op('pe', lambda e, g=g, dst=dst, o=o: e.matmul(dst[:, o:o + 192], lhsT=WfT[:, g, :], rhs=cs96[:], start=True, stop=True), reads=['WfT', 'cs96'], writes=[dn], signal=(g % 2 == 1))
                    T.op('dve', lambda e, k=k: e.tensor_copy(out=Wp[:, k, 0:384], in_=pA[:, 0:384]), reads=['pA'], writes=['Wp'])
                    T.op('act', lambda e, k=k: e.activation(out=Wp[:, k, 384:768], in_=pB[:, 0:384], func=AF.Copy), reads=['pB'], writes=['Wp'])
                for k in f1:
                    T.dma('pool', f1[k][:], cst[k][:, :], 'c_' + k, writes=[k])
                for k in tw:
                    T.dma('pool', tw[k][:], cst[k].rearrange("p a b c -> p (a b c)"), 'c_' + k, writes=[k])
                T.dma('pool', c2[:], cst['c2'][:, :], 'c_c2', writes=['c2'])
                T.dma('pool', s2[:], cst['s2'][:, :], 'c_s2', writes=['s2'])
                T.emit()
            xe_l = [xe[0], xe[1], sb1("xe2", [128, D], F32), sb1("xe3", [128, D], F32)]

            xp_c = xp.rearrange("(r c) d -> c r d", c=64)
            xs_f = xs.rearrange("b (r c) d -> c (b r) d", c=64)
            for kind in kinds:
                for c in range(ncol):
                    xt = xe_l[c % 4]
                    xslot = 'xe%d' % (c % 4)
                    if kind == 'p':
                        T.dma('sp', xt[:], xp_c[c], xslot + 'a', writes=[xslot + 'a', xslot + 'b'])
                    else:
                        T.dma('sp', xt[:], xs_f[c], xslot + 'a', writes=[xslot + 'a', xslot + 'b'])
                    par = c % 2
                    hT1 = hT1_l[par]
                    h1n = 'hT1_%d' % par
                    bkT, bkTn = (pT, 'pT') if par == 0 else (pT2, 'pS0')
                    bkA, bkAn = (pA, 'pA') if par == 0 else (pS[1], 'pS1')
                    bkB, bkBn = (pB, 'pB') if par == 0 else (pS[2], 'pS2')
                    bkO, bkOn = (pO, 'pO') if par == 0 else (pS[3], 'pS3')
                    rms_to_hT(xt, [xslot + 'a', xslot + 'b'], hT1[:], h1n, par=par, ptv=bkT, ptn=bkTn)
                    for hb, (dst, dn) in enumerate(((bkB, bkBn), (bkO, bkOn))):
                        for k in range(8):
                            T.op('pe', lambda e, k=k, hb=hb, dst=dst, hT1=hT1: e.matmul(dst[:, 0:384], lhsT=hT1[:, k, :], rhs=Wp[:, k, hb * 384:(hb + 1) * 384], start=(k == 0), stop=(k == 7)),
                                 reads=[h1n, 'Wp'], writes=[dn], signal=(k == 7))
                    for half, src, sname in ((0, bkB, bkBn), (1, bkO, bkOn)):
                        v = src[:, 0:384].rearrange("p (g ri c) -> p g ri c", g=2, ri=2)
                        for g in range(2):
                            j0_ = half * 96 + g * 48
                            for ri, (Dt, dn) in enumerate(((Dre, 'Dre'), (Dim, 'Dim'))):
                                if half == 0:
                                    T.op('dve', lambda e, v=v, g=g, j0_=j0_, c=c, ri=ri, Dt=Dt: e.tensor_copy(out=Dt[:, j0_:j0_ + 48, 2 * c:2 * c + 2], in_=v[:, g, ri, :].rearrange("p (j l) -> p j l", l=2)), reads=[sname], writes=[dn + '_w%d_%d' % (half, c % 2)])
                                else:
                                    T.op('act', lambda e, v=v, g=g, j0_=j0_, c=c, ri=ri, Dt=Dt: e.activation(out=Dt[:, j0_:j0_ + 48, 2 * c:2 * c + 2], in_=v[:, g, ri, :].rearrange("p (j l) -> p j l", l=2), func=AF.Copy), reads=[sname], writes=[dn + '_w%d_%d' % (half, c % 2)])
                fa, fb = f1['f1a_' + kind], f1['f1b_' + kind]
                tcc, tsp = tw['tcc_' + kind], tw['tsp_' + kind]
                tcc_v = tcc[:].rearrange("p (a b c) -> p a b c", a=2, b=2)
                tsp_v = tsp[:].rearrange("p (a b c) -> p a b c", a=2, b=2)
                for j4 in range(nj4):
                    p2, p2n = (pS[2], 'pS2') if j4 % 2 == 0 else (pS[3], 'pS3')
                    for jj in range(2):
                        pb = pS[jj]
                        pbn = 'pS%d' % jj
                        bi_ = (j4 % 2) * 2 + jj
                        tA, tB = tAs[bi_], tBs[bi_]
                        tan, tbn = 'tA%d' % bi_, 'tB%d' % bi_
                        for pr in range(2):
                            j = j4 * 4 + jj * 2 + pr
                            T.op('pe', lambda e, j=j, pr=pr, pb=pb, fa=fa: e.matmul(pb[:, pr * 256:(pr + 1) * 256], lhsT=Dre[:, j, :], rhs=fa[:], start=True, stop=False),
                                 reads=['Dre_w0_0', 'Dre_w0_1', 'Dre_w1_0', 'Dre_w1_1', 'f1a_' + kind], writes=[pbn], signal=False)
                            T.op('pe', lambda e, j=j, pr=pr, pb=pb, fb=fb: e.matmul(pb[:, pr * 256:(pr + 1) * 256], lhsT=Dim[:, j, :], rhs=fb[:], start=False, stop=True),
                                 reads=['Dim_w0_0', 'Dim_w0_1', 'Dim_w1_0', 'Dim_w1_1', 'f1b_' + kind], writes=[pbn], signal=(pr == 1))
                        psb = Psb[jj]
                        psn = "Psb%d" % jj
                        T.op('act', lambda e, pb=pb, psb=psb: e.activation(out=psb[:], in_=pb[:], func=AF.Copy), reads=[pbn], writes=[psn])
                        pv = psb[:].rearrange("p (a b c) -> p a b c", a=2, b=2)
                        tAv = tA[:].rearrange("p (a b c) -> p a b c", a=2, b=2)
                        tBv = tB[:].rearrange("p (a b c) -> p a b c", a=2, b=2)
                        T.op('dve', lambda e, psb=psb, tcc=tcc, tA=tA: e.tensor_tensor(out=tA[:], in0=psb[:], in1=tcc[:], op=ALU.mult), reads=[psn, 'tcc_' + kind], writes=[tan])
                        T.op('dve', lambda e, pv=pv, tBv=tBv, tsp_v=tsp_v: e.tensor_tensor(out=tBv[:, :, 0, :], in0=pv[:, :, 1, :], in1=tsp_v[:, :, 0, :], op=ALU.mult), reads=[psn, 'tsp_' + kind], writes=[tbn])
                        T.op('dve', lambda e, pv=pv, tBv=tBv, tsp_v=tsp_v: e.tensor_tensor(out=tBv[:, :, 1, :], in0=pv[:, :, 0, :], in1=tsp_v[:, :, 1, :], op=ALU.mult), reads=[psn, 'tsp_' + kind], writes=[tbn])
                        for pr in range(2):
                            q4 = jj * 2 + pr
                            srcs = [(tAv[:, pr, 0, :], c2, tan), (tBv[:, pr, 0, :], c2, tbn), (tAv[:, pr, 1, :], s2, tan), (tBv[:, pr, 1, :], s2, tbn)]
                            for si, (lh, rh, ln) in enumerate(srcs):
                                T.op('pe', lambda e, lh=lh, rh=rh, q4=q4, p2=p2, si=si: e.matmul(p2[:, q4 * 128:(q4 + 1) * 128], lhsT=lh, rhs=rh[:], start=(si == 0), stop=(si == 3)),
                                     reads=[ln, 'c2', 's2'], writes=[p2n], signal=(pr == 1 and si == 3))
                    T.op('act', lambda e, j4=j4, p2=p2: e.activation(out=Ybuf[:, :, 8 * j4:8 * j4 + 8].rearrange("p k (q l) -> p q k l", q=4),
                                                               in_=p2[:].rearrange("p (q k l) -> p q k l", q=4, k=64), func=AF.Copy), reads=[p2n], writes=['Ybuf'])
                for q8 in range(8 if do_yd else 0):
                    if kind == 'p':
                        T.dma('sp', yd[0:8192, :].rearrange("(k2 k1) ch -> k1 k2 ch", k1=128)[:, q8 * 8:(q8 + 1) * 8, :], Ybuf[:, q8 * 8:(q8 + 1) * 8, :], 'yd', reads=['Ybuf'], writes=['yd%s%d' % (kind, q8)])
                    else:
                        for b in range(2):
                            T.dma('sp', yd[8192 + 4096 * b:8192 + 4096 * (b + 1), :].rearrange("(k2 k1) ch -> k1 k2 ch", k1=64)[:, q8 * 8:(q8 + 1) * 8, :], Ybuf[64 * b:64 * b + 64, q8 * 8:(q8 + 1) * 8, :], 'yd', reads=['Ybuf'], writes=['yd%s%d%d' % (kind, q8, b)])
            T.barrier()
            T.emit()
        if stop == 1:
            return nc
        xn_l[1] = xn_l[0]
        xn_names[1] = 'xn0'

        with ExitStack() as es2:
            sb2 = lambda name, shape, dt: es2.enter_context(nc.sbuf_tensor(name, shape, dt))
            Win = sb2("Win", [128, 8, 5504], BF16)
            Wf = sb2("Wf", [128, 3, D], BF16)
            Wn = sb2("Wn", [128, 3, D], BF16)
            Wc = sb2("Wc", [128, 2, D], BF16)
            Wo = sb2("Wo", [128, 8, D], BF16)
            Tint = sb2("Tint", [128, 6, 10, 64], BF16)
            Tful = sb2("Tful", [128, 6, 14, 64], BF16)
            gf_t = sb2("gf_t", [128, D], F32)
            KmT3 = [sb2("KmT%d" % i, [128, 2, 256], BF16) for i in range(3)]
            Vm3 = [sb2("Vm%d" % i, [128, 2, 4, 65], BF16) for i in range(3)]
            with ExitStack() as es2a:
                sb2a = lambda name, shape, dt: es2a.enter_context(nc.sbuf_tensor(name, shape, dt))
                stg = sb2a("stg2", [128, 2752], F32)
                stg_h[0] = stg
                stg_h[1] = sb2a("stg2b", [128, 2752], F32)
                stg_h[2] = sb2a("stg2c", [128, 2752], F32)
                n_stg[0] = 3
                stg_w[0] = 2752
                Wkv_t = sb2a("Wkv", [128, 8, 512], BF16)
                memT = sb2a("memT", [128, 8, 256], BF16)
                load_weight(Wkv_t, w_kv, 8, 512, 'Wkv', scale_t=gmm_t)
                load_weight(Win, w_in, 8, 5504, 'Win', scale_t=gn_t, col0=384)
                load_weight(Wf, w_f, 3, D, 'Wf', scale_c=0.5)
                load_weight(Wn, w_n, 3, D, 'Wn', scale_c=0.5)
                load_weight(Wc, w_c, 2, D, 'Wc', scale_c=0.5)
                load_weight(Wo, w_o, 8, D, 'Wo', scale_c=0.5)
                T.dma('sp', gf_t[:], gfin[:, :], 'gf', writes=['gf'])
                for nm, tab, nj in (('int', Tint, 10), ('full', Tful, 14)):
                    n = 6 * nj * 64
                    tv = tab[:].rearrange("p h j q -> p (h j q)")
                    o = 0
                    while o < n:
                        w = min(1376, n - o)
                        bi = lw_ctr[0] % n_stg[0]
                        lw_ctr[0] += 1
                        sg_ = stg_h[bi]
                        sn = 'stg%d' % bi
                        T.dma('sp', sg_[:, 0:w], eb['eb_' + nm][:, o:o + w], sn, writes=[sn])
                        T.dma('sp', sg_[:, 1376:1376 + w], eb['em_' + nm][:, o:o + w], sn, writes=[sn])
                        T.op('act', lambda e, w=w, sg_=sg_: e.activation(out=sg_[:, 0:w], in_=sg_[:, 0:w], func=AF.Exp), reads=[sn], writes=[sn])
                        T.op('dve', lambda e, w=w, o=o, tv=tv, sg_=sg_: e.tensor_tensor(out=tv[:, o:o + w], in0=sg_[:, 0:w], in1=sg_[:, 1376:1376 + w], op=ALU.mult), reads=[sn], writes=['T' + nm])
                        o += w
                for mi in range(3):
                    KmT, Vm = KmT3[mi], Vm3[mi]
                    T.op('pool', lambda e, Vm=Vm: e.memset(Vm[:, :, :, 64:65], 1.0), writes=['Vm%d' % mi])
                    for mt in range(2):
                        xt = xe[mt % 2]
                        xslot = 'xe%d' % (mt % 2)
                        T.dma('sp', xt[:], mem[mi, mt * 128:(mt + 1) * 128, :], xslot, writes=[xslot])
                        rms_to_hT(xt, xslot, memT[:, :, mt * 128:(mt + 1) * 128], 'memT', par=mt % 2)
                    for hp in range(2):
                        for k in range(8):
                            T.op('pe', lambda e, hp=hp, k=k: e.matmul(pA[:, 0:256], lhsT=Wkv_t[:, k, hp * 128:(hp + 1) * 128], rhs=memT[:, k, :], start=(k == 0), stop=(k == 7)),
                                 reads=['memT', 'Wkv_a', 'Wkv_d'], writes=['pA'], signal=(k == 7))
                        T.op('act', lambda e, hp=hp, KmT=KmT: e.activation(out=KmT[:, hp, :], in_=pA[:, 0:256], func=AF.Copy), reads=['pA'], writes=['KmT%d' % mi])
                    for ck in range(2):
                        for k in range(8):
                            T.op('pe', lambda e, ck=ck, k=k: e.matmul(pB[:, 0:256], lhsT=memT[:, k, ck * 128:(ck + 1) * 128], rhs=Wkv_t[:, k, 256:512], start=(k == 0), stop=(k == 7)),
                                 reads=['memT', 'Wkv_a', 'Wkv_d'], writes=['pB'], signal=(k == 7))
                        T.op('dve', lambda e, ck=ck, Vm=Vm: e.tensor_copy(out=Vm[:, ck, :, 0:64], in_=pB[:, 0:256].rearrange("p (h d) -> p h d", h=4)), reads=['pB'], writes=['Vm%d' % mi])
                T.barrier()
                T.emit()
            if stop == 2:
                return nc
            NH, NR = 4, 6
            hT = [sb2("hT%d" % i, [128, 8, 128], BF16) for i in range(NH)]
            kT = [sb2("kT%d" % i, [128, 3, 128], BF16) for i in range(NR)]
            Vr = [sb2("Vr%d" % i, [128, 6, 65], BF16) for i in range(NR)]
            Pn = [sb2("Pn%d" % i, [128, 6, 128], BF16) for i in range(5)]
            qT = sb2("qT", [128, 3, 128], BF16)
            qcT = sb2("qcT", [128, 2, 128], BF16)
            Pc = sb2("Pc", [128, 8, 128], BF16)
            th = sb2("th", [128, 384], F32)
            rden = sb2("rden", [128, 6], F32)
            Nn = sb2("Nn", [128, 384], F32)
            act_b = sb2("act_b", [128, 384], BF16)
            BT = [{'f': sb2("FT%d" % i, [128, 3, 128], BF16), 'n': sb2("NT%d" % i, [128, 3, 128], BF16), 'c': sb2("CT%d" % i, [128, 2, 128], BF16)} for i in range(2)]
            th2s = [sb2("th2_%d" % i, [128, 512], F32) for i in range(2)]
            ss2 = sb2("ss2", [128, 1], F32)
            ssp2 = sb2("ssp2", [128, 1], F32)
            rstd2 = sb2("rstd2", [128, 1], F32)
            ybuf = sb2("ybuf", [128, 384], BF16)
            acc = sb2("acc", [128, 512], F32)
            mrg = sb2("mrg", [128, D], BF16)
            mT = sb2("mT", [128, 8, 128], BF16)
            xl = sb2("xl", [128, D], F32)
            for i in range(NR):
                T.op('pool', lambda e, i=i: e.memset(Vr[i][:, :, 64:65], 1.0), writes=['Vr%d' % i])

            def proj_tok(dstbank, bname, hslot, hTt, col, n):
                for k in range(8):
                    T.op('pe', lambda e, k=k: e.matmul(dstbank[:, 0:n], lhsT=hTt[:, k, :], rhs=Win[:, k, col:col + n], start=(k == 0), stop=(k == 7)),
                         reads=[hslot, 'Win'], writes=[bname], signal=(k == 7))

            def proj_feat(dstbank, bname, hslot, hTt, col, nch, ntok=128):
                for ch in range(nch):
                    for k in range(8):
                        T.op('pe', lambda e, k=k, ch=ch: e.matmul(dstbank[:, ch * ntok:(ch + 1) * ntok], lhsT=Win[:, k, col + ch * 128:col + (ch + 1) * 128], rhs=hTt[:, k, :], start=(k == 0), stop=(k == 7)),
                             reads=[hslot, 'Win'], writes=[bname], signal=(k == 7 and ch == nch - 1))

            def silu_gate(bank, bname, n):
                T.op('act', lambda e: e.activation(out=th[:, 0:n], in_=bank[:, 0:n], func=AF.Tanh, scale=0.5), reads=[bname], writes=['th'])
                T.op('dve', lambda e: e.scalar_tensor_tensor(out=th[:, 0:n], in0=th[:, 0:n], scalar=1.0, in1=bank[:, 0:n], op0=ALU.add, op1=ALU.mult), reads=['th', bname], writes=['th'])

            def to_featT(nchunk, dst, dname):
                for ch in range(nchunk):
                    T.op('pe', lambda e, ch=ch: e.transpose(out=pT[:, ch, :], in_=act_b[:, ch * 128:(ch + 1) * 128], identity=ident[:]), reads=['act_b', 'ident'], writes=['pT'], signal=(ch == nchunk - 1))
                T.op('act', lambda e: e.activation(out=dst[:], in_=pT[:, 0:nchunk, :], func=AF.Copy), reads=['pT'], writes=[dname])

            seqs = [(xp, yp, 0, 64, 0), (xs[0], ys[0], 8192, 32, 1), (xs[1], ys[1], 12288, 32, 2)]
            def run_seq(xa, ya, ydo, nt, mi):
                KmT, Vm, vmname, kmname = KmT3[mi], Vm3[mi], 'Vm%d' % mi, 'KmT%d' % mi

                def front(t):
                    xt = xe[t % 2]
                    xslot = 'xe%d' % (t % 2)
                    T.dma('sp', xt[:], xa[t * 128:(t + 1) * 128, :], xslot, writes=[xslot])
                    hs = 'hT%d' % (t % NH)
                    rms_to_hT(xt, xslot, hT[t % NH][:], hs, par=t % 2)
                    yield
                    proj_feat(pTf, 'pT', hs, hT[t % NH], OFF['k'], 3)
                    T.op('act', lambda e, t=t: e.activation(out=kT[t % NR][:].rearrange("p c t -> p (c t)"), in_=pTf[:, 0:384], func=AF.Copy), reads=['pT'], writes=['kT%d' % (t % NR)])
                    yield
                    proj_tok(pTf, 'pT', hs, hT[t % NH], OFF['v'], 384)
                    T.op('dve', lambda e, t=t: e.tensor_copy(out=Vr[t % NR][:, :, 0:64], in_=pTf[:, 0:384].rearrange("p (h d) -> p h d", h=6)), reads=['pT'], writes=['Vr%d' % (t % NR)])
                    yield

                def streamX(t):
                    hs = 'hT%d' % (t % NH)
                    h_ = hT[t % NH]
                    bt = BT[t % 2]
                    btn = 'BT%d' % (t % 2)
                    proj_feat(pA, 'pA', hs, h_, OFF['q'], 3)
                    T.op('act', lambda e: e.activation(out=qT[:].rearrange("p c t -> p (c t)"), in_=pA[:, 0:384], func=AF.Copy), reads=['pA'], writes=['qT'])
                    yield
                    if 2 <= t <= nt - 3:
                        keys = [(tau, Tint, 'Tint', 4 - 2 * (tau - t)) for tau in range(t - 2, t + 3)]
                    elif t < 2:
                        keys = [(tau, Tful, 'Tfull', 6 - 2 * (tau - t)) for tau in range(0, 4)]
                    else:
                        keys = [(tau, Tful, 'Tfull', 6 - 2 * (tau - t)) for tau in range(nt - 4, nt)]
                    for i, (tau, tab, tname, j0) in enumerate(keys):
                        kt = kT[tau % NR]
                        kname = 'kT%d' % (tau % NR)
                        b0, b1 = pS[(2 * i) % 3], pS[(2 * i + 1) % 3]
                        n0, n1 = 'pS%d' % ((2 * i) % 3), 'pS%d' % ((2 * i + 1) % 3)
                        for h in (0, 2, 4, 1, 3, 5):
                            bk = b0 if h % 2 == 0 else b1
                            T.op('pe', lambda e, h=h, bk=bk, kt=kt: e.matmul(bk[:, (h // 2) * 128:(h // 2 + 1) * 128], lhsT=kt[64 * (h % 2):64 * (h % 2) + 64, h // 2, :], rhs=qT[64 * (h % 2):64 * (h % 2) + 64, h // 2, :], start=True, stop=True),
                                 reads=[kname, 'qT'], writes=[n0 if h % 2 == 0 else n1], signal=(h == 5))
                        P = Pn[i]
                        pname = 'Pn%d' % i
                        T.op('act', lambda e, P=P, b0=b0: e.activation(out=P[:, 0:3, :].rearrange("p h q -> p (h q)"), in_=b0[:, 0:384], func=AF.Exp, scale=0.125), reads=[n0], writes=[pname])
                        T.op('act', lambda e, P=P, b1=b1: e.activation(out=P[:, 3:6, :].rearrange("p h q -> p (h q)"), in_=b1[:, 0:384], func=AF.Exp, scale=0.125), reads=[n1, pname], writes=[pname])
                        T.op('dve', lambda e, P=P, tab=tab, j0=j0: e.tensor_tensor(out=P[:].rearrange("p h (a q) -> p h a q", a=2), in0=P[:].rearrange("p h (a q) -> p h a q", a=2), in1=tab[:, :, j0:j0 + 2, :], op=ALU.mult),
                             reads=[pname, tname], writes=[pname])
                        yield
                    nk = len(keys)
                    for h in range(6):
                        for i, (tau, tab, tname, j0) in enumerate(keys):
                            T.op('pe', lambda e, h=h, i=i, tau=tau: e.matmul(pO[:, h * 65:(h + 1) * 65], lhsT=Pn[i][:, (h % 2) * 3 + h // 2, :], rhs=Vr[tau % NR][:, h, :], start=(i == 0), stop=(i == nk - 1)),
                                 reads=['Pn%d' % i, 'Vr%d' % (tau % NR)], writes=['pO'], signal=(i == nk - 1 and h == 5))
                    proj_tok(pA, 'pA', hs, h_, OFF['gna'], 384)
                    silu_gate(pA, 'pA', 384)
                    yield
                    pOv = pO[:, 0:390].rearrange("p (h d) -> p h d", h=6)
                    T.op('dve', lambda e: e.reciprocal(out=rden[:, 0:6], in_=pOv[:, :, 64]), reads=['pO'], writes=['rden'])
                    T.op('dve', lambda e: e.tensor_tensor(out=Nn[:, 0:384].rearrange("p (h d) -> p h d", h=6), in0=pOv[:, :, 0:64], in1=rden[:, 0:6].unsqueeze(2).broadcast_to([128, 6, 64]), op=ALU.mult), reads=['pO', 'rden'], writes=['Nn'])
                    T.op('dve', lambda e: e.tensor_tensor(out=act_b[:, 0:384], in0=Nn[:, 0:384], in1=th[:, 0:384], op=ALU.mult), reads=['Nn', 'th'], writes=['act_b'])
                    to_featT(3, bt['n'], btn + 'n')
                    yield
                    proj_feat(pA, 'pA', hs, h_, OFF['qca'], 2)
                    T.op('act', lambda e: e.activation(out=qcT[:].rearrange("p c t -> p (c t)"), in_=pA[:, 0:256], func=AF.Copy), reads=['pA'], writes=['qcT'])
                    yield
                    for par in range(2):
                        for h in (par, par + 2):
                            for ck in range(2):
                                sl = (h // 2) * 2 + ck
                                T.op('pe', lambda e, ck=ck, h=h, sl=sl, par=par: e.matmul(pS[par][:, sl * 128:(sl + 1) * 128], lhsT=KmT[64 * (h % 2):64 * (h % 2) + 64, h // 2, ck * 128:(ck + 1) * 128], rhs=qcT[64 * (h % 2):64 * (h % 2) + 64, h // 2, :], start=True, stop=True),
                                     reads=[kmname, 'qcT'], writes=['pS%d' % par], signal=(sl == 3))
                        T.op('act', lambda e, par=par: e.activation(out=Pc[:, par * 4:(par + 1) * 4, :].rearrange("p h q -> p (h q)"), in_=pS[par][:, :], func=AF.Exp, scale=0.125), reads=['pS%d' % par, 'Pc'], writes=['Pc'])
                    yield
                    for h in range(4):
                        for ck in range(2):
                            T.op('pe', lambda e, h=h, ck=ck: e.matmul(pS[2][:, h * 65:(h + 1) * 65], lhsT=Pc[:, (h % 2) * 4 + (h // 2) * 2 + ck, :], rhs=Vm[:, ck, h, :], start=(ck == 0), stop=(ck == 1)),
                                 reads=['Pc', vmname], writes=['pS2'], signal=(ck == 1 and h == 3))
                    proj_tok(pS[0], 'pS0', hs, h_, OFF['gca'], 256)
                    silu_gate(pS[0], 'pS0', 256)
                    yield
                    pBv = pS[2][:, 0:260].rearrange("p (h d) -> p h d", h=4)
                    T.op('dve', lambda e: e.reciprocal(out=rden[:, 0:4], in_=pBv[:, :, 64]), reads=['pS2'], writes=['rden'])
                    T.op('dve', lambda e: e.tensor_tensor(out=Nn[:, 0:256].rearrange("p (h d) -> p h d", h=4), in0=pBv[:, :, 0:64], in1=rden[:, 0:4].unsqueeze(2).broadcast_to([128, 4, 64]), op=ALU.mult), reads=['pS2', 'rden'], writes=['Nn'])
                    T.op('dve', lambda e: e.tensor_tensor(out=act_b[:, 0:256], in0=Nn[:, 0:256], in1=th[:, 0:256], op=ALU.mult), reads=['Nn', 'th'], writes=['act_b'])
                    to_featT(2, bt['c'], btn + 'c')
                    yield
                    T.dma('sp', ybuf[:], yd[ydo + t * 128:ydo + (t + 1) * 128, :], 'ybuf', writes=['ybuf'])
                    proj_tok(pO, 'pO', hs, h_, OFF['gf'], 384)
                    silu_gate(pO, 'pO', 384)
                    T.op('dve', lambda e: e.tensor_tensor(out=act_b[:, 0:384], in0=ybuf[:, :], in1=th[:, 0:384], op=ALU.mult), reads=['ybuf', 'th'], writes=['act_b'])
                    to_featT(3, bt['f'], btn + 'f')
                    yield

                def streamY(t):
                    hs = 'hT%d' % (t % NH)
                    h_ = hT[t % NH]
                    bt = BT[t % 2]
                    btn = 'BT%d' % (t % 2)
                    T.dma('sp', xl[:], xa[t * 128:(t + 1) * 128, :], 'xl', writes=['xl'])
                    for half in range(2):
                        for bi, (bk, W_, nkc) in enumerate((('f', Wf, 3), ('n', Wn, 3), ('c', Wc, 2))):
                            col = GM_OFF + bi * 1024 + half * 512
                            th2 = th2s[(half * 3 + bi) % 2]
                            thn = 'th2_%d' % ((half * 3 + bi) % 2)
                            proj_tok(pB, 'pB', hs, h_, col, 512)
                            T.op('act', lambda e, th2=th2: e.activation(out=th2[:, :], in_=pB[:, :], func=AF.Tanh, scale=0.5), reads=['pB'], writes=[thn])
                            for kc in range(nkc):
                                T.op('pe', lambda e, kc=kc, bk=bk, W_=W_, half=half, bt=bt, nkc=nkc: e.matmul(pS[3][:, :], lhsT=bt[bk][:, kc, :], rhs=W_[:, kc, half * 512:(half + 1) * 512], start=(kc == 0), stop=(kc == nkc - 1)),
                                     reads=[btn + bk, 'W' + bk], writes=['pS3'], signal=(kc == nkc - 1))
                            yield
                            if bi == 0:
                                T.op('dve', lambda e, th2=th2: e.scalar_tensor_tensor(out=acc[:, :], in0=th2[:, :], scalar=1.0, in1=pS[3][:, :], op0=ALU.add, op1=ALU.mult), reads=[thn, 'pS3'], writes=['acc'])
                            else:
                                T.op('dve', lambda e, th2=th2: e.scalar_tensor_tensor(out=th2[:, :], in0=th2[:, :], scalar=1.0, in1=pS[3][:, :], op0=ALU.add, op1=ALU.mult), reads=[thn, 'pS3'], writes=[thn])
                                if bi == 1:
                                    T.op('dve', lambda e, th2=th2: e.tensor_tensor(out=acc[:, :], in0=acc[:, :], in1=th2[:, :], op=ALU.add), reads=['acc', thn], writes=['acc'])
                                else:
                                    T.op('dve', lambda e, half=half, th2=th2: e.tensor_tensor(out=mrg[:, half * 512:(half + 1) * 512], in0=acc[:, :], in1=th2[:, :], op=ALU.add), reads=['acc', thn], writes=['mrg'])
                    yield
                    for k in range(8):
                        T.op('pe', lambda e, k=k: e.transpose(out=pT[:, k, :], in_=mrg[:, k * 128:(k + 1) * 128], identity=ident[:]), reads=['mrg', 'ident'], writes=['pT'], signal=(k == 7))
                    T.op('act', lambda e: e.activation(out=mT[:].rearrange("p k t -> p (k t)"), in_=pT[:].rearrange("p k t -> p (k t)"), func=AF.Copy), reads=['pT'], writes=['mT'])
                    yield
                    for half in range(2):
                        ob, obn = (pB, 'pB') if half == 0 else (pS[3], 'pS3')
                        for k in range(8):
                            T.op('pe', lambda e, k=k, half=half, ob=ob: e.matmul(ob[:, :], lhsT=mT[:, k, :], rhs=Wo[:, k, half * 512:(half + 1) * 512], start=(k == 0), stop=(k == 7)),
                                 reads=['mT', 'Wo'], writes=[obn], signal=(k == 7))
                    for half in range(2):
                        ob, obn = (pB, 'pB') if half == 0 else (pS[3], 'pS3')
                        T.op('dve', lambda e, half=half, ob=ob: e.tensor_tensor(out=xl[:, half * 512:(half + 1) * 512], in0=xl[:, half * 512:(half + 1) * 512], in1=ob[:, :], op=ALU.add), reads=['xl', obn], writes=['xl'])
                    if True:
                        yield
                    T.op('act', lambda e: e.activation(out=mrg[:], in_=xl[:], func=AF.Square, scale=1.0 / 32, accum_out=ss2[:]), reads=['xl'], writes=['mrg', 'ss2'])
                    T.op('dve', lambda e: e.tensor_scalar(out=ssp2[:], in0=ss2[:], scalar1=EPS, scalar2=None, op0=ALU.add), reads=['ss2'], writes=['ssp2'])
                    T.op('pool', lambda e: e.tensor_tensor(out=rstd2[:], in0=ssp2[:], in1=mh[:], op=ALU.pow), reads=['ssp2', 'mh'], writes=['rstd2'])
                    T.op('dve', lambda e: e.scalar_tensor_tensor(out=xl[:], in0=xl[:], scalar=rstd2[:, 0:1], in1=gf_t[:], op0=ALU.mult, op1=ALU.mult), reads=['xl', 'rstd2', 'gf'], writes=['xl'])
                    T.dma('pool', ya[t * 128:(t + 1) * 128, :], xl[:], 'out', reads=['xl'])
                    yield

                def interleave(specs):
                    live = [[g, d] for g, d in specs if g is not None]
                    rnd = 0
                    while live:
                        for it in list(live):
                            if it[1] > rnd:
                                continue
                            try:
                                next(it[0])
                            except StopIteration:
                                live.remove(it)
                        rnd += 1

                def run(g):
                    for _ in g:
                        pass

                for tt in range(3):
                    run(front(tt))
                for t in range(nt):
                    if t + 3 < nt:
                        run(front(t + 3))
                    run(streamX(t))
                    run(streamY(t))

            for sq_ in seqs:
                if ntl is not None:
                    sq_ = (sq_[0], sq_[1], sq_[2], ntl, sq_[4])
                run_seq(*sq_)
            T.barrier()
            T.emit()
    nc._trk_stats = T.stats
    return nc


_NC_CACHE = {}


def kernel(x_prompt, x_sample, mem_prompt, mem_sample, g_norm, w_in, na_rpb, g_mem, w_mem_kv,
           w_f_out, w_na_out, w_ca_out, w_out, g_final):
    f = lambda a: np.ascontiguousarray(np.asarray(a, dtype=np.float32))
    x_prompt, x_sample, mem_prompt, mem_sample = f(x_prompt), f(x_sample), f(mem_prompt), f(mem_sample)
    if 'nc' not in _NC_CACHE:
        _NC_CACHE['nc'] = build_nc()
    nc = _NC_CACHE['nc']
    shared = {
        "gn": f(np.asarray(g_norm)[0].reshape(8, 128).T),
        "gmm": f(np.asarray(g_mem)[0].reshape(8, 128).T),
        "gfin": f(np.broadcast_to(np.asarray(g_final)[None, :], (128, D))),
        "w_in": f(np.asarray(w_in)[0]), "w_kv": f(np.asarray(w_mem_kv)[0]), "w_f": f(np.asarray(w_f_out)[0]),
        "w_n": f(np.asarray(w_na_out)[0]), "w_c": f(np.asarray(w_ca_out)[0]), "w_o": f(np.asarray(w_out)[0]),
    }
    for k, v in _consts().items():
        shared["c_" + k] = f(v)
    shared.update(_build_etabs(np.asarray(na_rpb, dtype=np.float32)[0]))
    in_maps = []
    for i in range(NCORES):
        m = dict(shared)
        m["xp"] = x_prompt[i]
        m["xs"] = x_sample[2 * i:2 * i + 2]
        m["mem"] = f(np.concatenate([mem_prompt[i:i + 1], mem_sample[2 * i:2 * i + 2]], 0))
        in_maps.append(m)
    res = run_bass_kernel_spmd(nc, in_maps, core_ids=list(range(NCORES)))
    y_p = np.stack([np.asarray(r["yp"], dtype=np.float32) for r in res.results], 0)
    y_s = np.concatenate([np.asarray(r["ys"], dtype=np.float32) for r in res.results], 0)
    return (y_p, y_s)
```
